# Optimizing a Trainium2 kernel written in Bass

```python
import math
import jax, jax.numpy as jnp
from jax import lax
import numpy as np

D_MODEL = 1024
BATCH = 2
SEQ = 8192
DEPTH = 1

POOL_WIDTH = D_MODEL // 2
POOL_WINDOWS = (2, 4, 8, 16)
N_POOL_GROUPS = len(POOL_WINDOWS)
POOL_GROUP = POOL_WIDTH // N_POOL_GROUPS
ATTN_WIDTH = D_MODEL // 2
N_HEADS = 4
HEAD_DIM = ATTN_WIDTH // (2 * N_HEADS)
V_DIM = 2 * HEAD_DIM
Q_BLOCK = 128
N_BRANCHES = 2
D_FF = 4 * D_MODEL
IN_WIDTH = POOL_WIDTH + 2 * ATTN_WIDTH + N_HEADS * V_DIM + N_BRANCHES * D_MODEL
EPS = 1e-6

kernel_name = "hybrid_pool_diffattn_gated_block"


def _alibi_slopes(n_heads):
    return jnp.asarray(np.array([2.0 ** (-8.0 * (h + 1) / n_heads) for h in range(n_heads)], dtype=np.float32))


def _lambda_init(layer_idx):
    return 0.8 - 0.6 * math.exp(-0.3 * layer_idx)


def rmsnorm(x, g):
    xf = x.astype(jnp.float32)
    y = xf * lax.rsqrt(jnp.mean(xf * xf, axis=-1, keepdims=True) + EPS)
    return (y * g.astype(jnp.float32)).astype(x.dtype)


def causal_multiscale_pool(u, pool_w, pool_scale):
    B, S, _ = u.shape
    ug = u.reshape(B, S, N_POOL_GROUPS, POOL_GROUP).astype(jnp.float32)
    cs = jnp.cumsum(ug, axis=1)
    cs = jnp.concatenate([jnp.zeros((B, 1, N_POOL_GROUPS, POOL_GROUP), jnp.float32), cs], axis=1)
    t = jnp.arange(S)
    means = []
    for g, w in enumerate(POOL_WINDOWS):
        lo = jnp.maximum(t + 1 - w, 0)
        cnt = (t + 1 - lo).astype(jnp.float32)
        means.append((cs[:, 1:, g] - cs[:, lo, g]) / cnt[None, :, None])
    pooled = jnp.stack(means, axis=2) - ug
    mixed = jnp.einsum('bsgc,gcd->bsgd', pooled.astype(u.dtype), pool_w)
    return mixed.reshape(B, S, POOL_WIDTH) * pool_scale


def differential_attention(q, k, v, lam, lambda_init, subln_g):
    B, S = q.shape[0], q.shape[1]
    nb = S // Q_BLOCK
    scale = 1.0 / math.sqrt(HEAD_DIM)
    slopes = _alibi_slopes(N_HEADS)
    kpos = jnp.arange(S)
    qb = q.reshape(B, nb, Q_BLOCK, N_HEADS, 2, HEAD_DIM).transpose(1, 0, 2, 3, 4, 5)

    def one_block(args):
        qi, bi = args
        qpos = bi * Q_BLOCK + jnp.arange(Q_BLOCK)
        s = jnp.einsum('bqhcd,bkhcd->bhcqk', qi, k, preferred_element_type=jnp.float32) * scale
        dist = (qpos[:, None] - kpos[None, :]).astype(jnp.float32)
        s = s - slopes[:, None, None, None] * dist[None, None]
        causal = kpos[None, :] <= qpos[:, None]
        s = jnp.where(causal[None, None, None], s, -jnp.inf)
        p = jax.nn.softmax(s, axis=-1)
        a = p[:, :, 0] - lam * p[:, :, 1]
        return jnp.einsum('bhqk,bkhv->bqhv', a.astype(v.dtype), v)

    o = lax.map(one_block, (qb, jnp.arange(nb)))
    o = o.transpose(1, 0, 2, 3, 4).reshape(B, S, N_HEADS, V_DIM)
    o = rmsnorm(o, subln_g) * (1.0 - lambda_init)
    return o.reshape(B, S, N_HEADS * V_DIM)


def setup_inputs(seed: int = 0) -> dict:
    key = jax.random.key(seed)
    ks = jax.random.split(key, 20)
    f32 = jnp.float32
    nrm = lambda k, shape, s: (jax.random.normal(k, shape, f32) * s).astype(f32)
    return {
        "x": nrm(ks[0], (BATCH, SEQ, D_MODEL), 1.0),
        "norm1_g": 1.0 + nrm(ks[1], (DEPTH, D_MODEL), 0.05),
        "w_in": nrm(ks[2], (DEPTH, D_MODEL, IN_WIDTH), D_MODEL ** -0.5),
        "gate_b": nrm(ks[3], (DEPTH, N_BRANCHES * D_MODEL), 0.02),
        "pool_w": nrm(ks[4], (DEPTH, N_POOL_GROUPS, POOL_GROUP, POOL_GROUP), POOL_GROUP ** -0.5),
        "pool_scale": 1.0 + nrm(ks[5], (DEPTH, POOL_WIDTH), 0.05),
        "q_norm_g": 1.0 + nrm(ks[6], (DEPTH, HEAD_DIM), 0.05),
        "k_norm_g": 1.0 + nrm(ks[7], (DEPTH, HEAD_DIM), 0.05),
        "lambda_q1": nrm(ks[8], (DEPTH, HEAD_DIM), 0.1),
        "lambda_k1": nrm(ks[9], (DEPTH, HEAD_DIM), 0.1),
        "lambda_q2": nrm(ks[10], (DEPTH, HEAD_DIM), 0.1),
        "lambda_k2": nrm(ks[11], (DEPTH, HEAD_DIM), 0.1),
        "subln_g": 1.0 + nrm(ks[12], (DEPTH, V_DIM), 0.05),
        "w_branch_a": nrm(ks[13], (DEPTH, POOL_WIDTH, D_MODEL), POOL_WIDTH ** -0.5),
        "w_branch_b": nrm(ks[14], (DEPTH, N_HEADS * V_DIM, D_MODEL), (N_HEADS * V_DIM) ** -0.5),
        "w_out": nrm(ks[15], (DEPTH, D_MODEL, D_MODEL), D_MODEL ** -0.5),
        "norm2_g": 1.0 + nrm(ks[16], (DEPTH, D_MODEL), 0.05),
        "w_ff1": nrm(ks[17], (DEPTH, D_MODEL, D_FF), D_MODEL ** -0.5),
        "w_ff2": nrm(ks[18], (DEPTH, D_FF, D_MODEL), D_FF ** -0.5),
    }


def reference(x, norm1_g, w_in, gate_b, pool_w, pool_scale, q_norm_g, k_norm_g,
              lambda_q1, lambda_k1, lambda_q2, lambda_k2, subln_g,
              w_branch_a, w_branch_b, w_out, norm2_g, w_ff1, w_ff2):
    B, S, D = x.shape
    splits = np.cumsum([POOL_WIDTH, ATTN_WIDTH, ATTN_WIDTH, N_HEADS * V_DIM]).tolist()
    for l in range(DEPTH):
        lambda_init = _lambda_init(l)
        h = rmsnorm(x, norm1_g[l])
        proj = h @ w_in[l]
        u, q, k, v, gl = jnp.split(proj, splits, axis=-1)
        ya = causal_multiscale_pool(u, pool_w[l], pool_scale[l]) @ w_branch_a[l]
        q = rmsnorm(q.reshape(B, S, N_HEADS, 2, HEAD_DIM), q_norm_g[l])
        k = rmsnorm(k.reshape(B, S, N_HEADS, 2, HEAD_DIM), k_norm_g[l])
        v = v.reshape(B, S, N_HEADS, V_DIM)
        lq1k1 = jnp.sum(lambda_q1[l].astype(jnp.float32) * lambda_k1[l].astype(jnp.float32))
        lq2k2 = jnp.sum(lambda_q2[l].astype(jnp.float32) * lambda_k2[l].astype(jnp.float32))
        lam = jnp.exp(lq1k1) - jnp.exp(lq2k2) + lambda_init
        yb = differential_attention(q, k, v, lam, lambda_init, subln_g[l]) @ w_branch_b[l]
        g = jax.nn.sigmoid((gl + gate_b[l]).astype(jnp.float32)).astype(x.dtype)
        g = g.reshape(B, S, N_BRANCHES, D)
        merged = g[:, :, 0] * ya + g[:, :, 1] * yb
        x = x + merged @ w_out[l]
        h2 = rmsnorm(x, norm2_g[l])
        x = x + jnp.square(jax.nn.relu(h2 @ w_ff1[l])) @ w_ff2[l]
    return x
```

```python
import math
from contextlib import ExitStack

import numpy as np
import concourse.bass as bass
import concourse.mybir as mybir
from concourse.bass_utils import run_bass_kernel_spmd
from concourse.alu_op_type import AluOpType as ALU

F32 = mybir.dt.float32
BF16 = mybir.dt.bfloat16
U8 = mybir.dt.uint8
AF = mybir.ActivationFunctionType

D = 1024
SEQ = 8192
CH = 512
NPOS = 16
NCORES = 8
SCALE = 0.125
EPS = 1e-6
SLOPES = [2.0 ** (-8.0 * (h + 1) / 4) for h in range(4)]
LAMBDA_INIT = 0.8 - 0.6 * math.exp(-0.3 * 0)
MAXOTH = [3, 6, 9, 12]
POOL_W = (2, 4, 8, 16)
NEG = -30000.0


def slot_units(s):
    return [("own", t) for t in range(s)] + [("oth", 4 + i) for i in range(MAXOTH[s])] + [("diag", s)]


UNIT_BASE = []
_c = 0
for _s in range(4):
    UNIT_BASE.append(_c)
    _c += len(slot_units(_s))
NUNITS = _c
NBC = NUNITS * 4 * 7


def bcol(s, u, t, h, jj=0):
    return ((UNIT_BASE[s] + u) * 4 + t) * 7 + (jj if h == 0 else 3 + h)


ENGS = ("sync", "act", "pool", "dve", "pe")


class Tok:
    __slots__ = ("name", "w", "r")

    def __init__(self, name=""):
        self.name = name
        self.w = None
        self.r = {}


class Ins:
    __slots__ = ("eng", "fn", "waits", "stream", "idx", "signal")

    def __init__(self, eng, fn, stream):
        self.eng = eng
        self.fn = fn
        self.waits = []
        self.stream = stream
        self.idx = None
        self.signal = False


class Prog:
    def __init__(self):
        self.q = {e: [] for e in ENGS}
        self.streams = {}
        self.waited = {e: {} for e in ENGS}
        self.pending = {e: [] for e in ENGS}

    def barrier(self):
        lasts = [st[-1] for st in self.streams.values() if st]
        for e in ENGS:
            self.pending[e] = list(lasts)

    def emit(self, eng, fn, reads=(), writes=(), stream=None):
        is_dma = stream is not None
        sname = stream if is_dma else eng
        ins = Ins(eng, fn, sname)
        st = self.streams.setdefault(sname, [])
        ins.idx = len(st)
        deps = []
        for t in reads:
            if t.w is not None:
                deps.append(t.w)
        for t in writes:
            if t.w is not None:
                deps.append(t.w)
            deps.extend(t.r.values())
        if self.pending[eng]:
            deps.extend(self.pending[eng])
            self.pending[eng] = []
        wd = self.waited[eng]
        need = {}
        for d in deps:
            if (not is_dma) and eng == "pe" and d.stream == "pe":
                continue
            if d.stream not in ENGS:
                d = self.streams[d.stream][-1]
            if wd.get(d.stream, -1) >= d.idx:
                continue
            if need.get(d.stream) is None or need[d.stream].idx < d.idx:
                need[d.stream] = d
        for sn, d in need.items():
            d.signal = True
            wd[sn] = d.idx
            ins.waits.append(d)
        st.append(ins)
        for t in reads:
            t.r[sname] = ins
        for t in writes:
            t.w = ins
            t.r = {}
        self.q[eng].append(ins)
        return ins

    def finalize(self, nc, sems, final_waits):
        val = {}
        for sname, st in self.streams.items():
            c = 0
            isdma = sname not in ENGS
            for ins in st:
                if isdma:
                    ins.signal = True
                if ins.signal:
                    c += 16 if isdma else 1
                val[id(ins)] = c
        engobj = {"sync": nc.sync, "act": nc.scalar, "pool": nc.gpsimd,
                  "dve": nc.vector, "pe": nc.tensor}

        def run(eng):
            e = engobj[eng]
            for ins in self.q[eng]:
                for d in ins.waits:
                    e.wait_ge(sems[d.stream], val[id(d)])
                r = ins.fn(e)
                if ins.signal:
                    r.then_inc(sems[ins.stream], 16 if ins.stream not in ENGS else 1)
            for d in final_waits.get(eng, ()):
                e.wait_ge(sems[d.stream], val[id(d)])
        return run


def I(name, *a, **k):
    return lambda e: getattr(e, name)(*a, **k)


class Alloc:
    def __init__(self, ranges):
        self.ranges = [[lo, hi] for lo, hi in ranges]

    def get(self, n):
        n = (n + 63) // 64 * 64
        for r in self.ranges:
            if r[1] - r[0] >= n:
                o = r[0]
                r[0] += n
                return o
        raise RuntimeError(f"arena OOM for {n}: {self.ranges}")


ARENA_BYTES = 212736
MEMORD = [0, 4, 5, 6, 1, 7, 8, 9, 2, 10, 11, 12, 3, 13, 14, 15]
PRECAST = True


def build_nc():
    nc = bass.Bass("TRN2", target_bir_lowering=False)

    def din(name, shape):
        return nc.dram_tensor(name, shape, F32, kind="ExternalInput").ap()

    xkv = din("xkv", [SEQ, D])
    xhalo = din("xhalo", [4 * 128, D])
    biastab = din("biastab", [128, NBC])
    corr = din("corr", [128, 64])
    w_in = din("w_in", [D, 4096])
    gate_b = din("gate_b", [128, 16])
    pool_w = din("pool_w", [512, 128])
    pool_scale = din("pool_scale", [128, 4])
    g1b = din("g1b", [128, D])
    g2b = din("g2b", [128, D])
    gq = din("gq", [128, 1])
    gk = din("gk", [128, 1])
    lam = din("lam", [128, 256])
    subg = din("subg", [128, 1])
    w_a = din("w_a", [512, D])
    w_b = din("w_b", [512, D])
    w_o = din("w_o", [D, D])
    w_1 = din("w_1", [D, 4096])
    w_2 = din("w_2", [4096, D])
    cst = din("cst", [128, 3 * 128])
    out = nc.dram_tensor("out", [2048, D], F32, kind="ExternalOutput").ap()
    x1s = nc.dram_tensor("x1s", [2048, D], F32, kind="Internal").ap()

    def dbf(name, shape):
        return nc.dram_tensor(name, shape, BF16, kind="Internal").ap()

    wu_b = dbf("wu_b", [D, 512]); wg_b = dbf("wg_b", [D, 2048]); wa_b = dbf("wa_b", [512, D])
    wb_b = dbf("wb_b", [512, D]); wo_b = dbf("wo_b", [D, D]); wp_b = dbf("wp_b", [512, 128])
    w1_b = dbf("w1_b", [D, 4096]); w2_b = dbf("w2_b", [4096, D])

    P = Prog()
    es = ExitStack()
    with es:
        arena = es.enter_context(nc.sbuf_tensor("arena", [128, ARENA_BYTES], U8))
        psum = es.enter_context(nc.psum_tensor("psum", [128, 4096], F32))
        stream_names = list(ENGS) + ["ld_xa", "ld_xb", "ld_w", "ld_c", "ld_h", "st_o", "st_x1", "ld_x1a", "ld_x1b",
                                     "ld_cp", "ld_wq", "ld_wv", "pc0", "pc1", "pc2", "pc3", "lw0", "lw1", "lw2"] + [f"lf{i}" for i in range(8)]
        sems = {n: es.enter_context(nc.semaphore(n)) for n in stream_names}
        block = es.enter_context(nc.Block())

        def view(off, shape, dt):
            esz = 4 if dt == F32 else 2
            n = 1
            for d_ in shape:
                n *= d_
            v = arena[:, off:off + n * esz].bitcast(dt)
            if len(shape) == 2:
                return v.rearrange("p (a b) -> p a b", b=shape[1])
            if len(shape) == 1:
                return v
            return v.rearrange("p (a b c) -> p a b c", b=shape[1], c=shape[2])

        def bank(i):
            return psum[:, 512 * i:512 * (i + 1)]

        tb = [Tok(f"bank{i}") for i in range(8)]
        rr = [0]

        def nb():
            i = rr[0]
            rr[0] = (i + 1) % 8
            return i

        A0 = Alloc([(0, ARENA_BYTES)])
        o_ident = A0.get(256); o_blk = A0.get(256)
        o_small = A0.get(64 * 4)
        ident = view(o_ident, [128], BF16)
        blk64 = view(o_blk, [128], BF16)
        small = view(o_small, [64], F32)
        t_const = Tok("const")
        t_small = Tok("small")
        for i, o_ in ((0, o_ident), (2, o_blk)):
            P.emit("pool", I("dma_start", out=view(o_, [128], BF16), in_=cst[:, 128 * i:128 * (i + 1)]),
                   writes=[t_const], stream="ld_cp")
        P.emit("sync", I("dma_start", out=small[:, 0:1], in_=gq), writes=[t_small], stream="ld_c")
        P.emit("sync", I("dma_start", out=small[:, 1:2], in_=gk), writes=[t_small], stream="ld_c")
        P.emit("sync", I("dma_start", out=small[:, 2:3], in_=subg), writes=[t_small], stream="ld_c")
        P.emit("sync", I("dma_start", out=small[:, 24:28], in_=pool_scale), writes=[t_small], stream="ld_c")
        CONST_END = A0.get(0)

        KV0 = A0.get(16 * 8192)
        KV_END = A0.get(0)
        o_QT = A0.get(4 * 2048 * 2)
        COMMON_END = A0.get(0)

        def kvbase(pos):
            return KV0 + MEMORD.index(pos) * 8192

        KTp = [view(kvbase(p), [4, 512], BF16) for p in range(NPOS)]
        Vp = [view(kvbase(p) + 4096, [4, 512], BF16) for p in range(NPOS)]
        QT = view(o_QT, [4, 2048], BF16)
        t_KT = [Tok(f"KT{p}") for p in range(NPOS)]
        t_V = [Tok(f"V{p}") for p in range(NPOS)]
        t_QT = [Tok(f"QT{p}") for p in range(4)]

        ssi = [0]

        def norm_A(xsrc, t_x, hb, t_hb, gbv_, t_gb):
            c = ssi[0] % 8
            ssi[0] += 1
            ss = small[:, 8 + c:9 + c]
            lnv = small[:, 16 + c:17 + c]
            t_ss = Tok("ss")
            P.emit("act", I("activation", out=hb, in_=xsrc, func=AF.Square, accum_out=ss),
                   reads=[t_x], writes=[t_hb, t_ss])
            P.emit("act", I("activation", out=lnv, in_=ss, func=AF.Ln, scale=1.0 / D, bias=EPS),
                   reads=[t_ss], writes=[t_ss])
            P.emit("act", I("activation", out=lnv, in_=lnv, func=AF.Exp, scale=-0.5),
                   reads=[t_ss], writes=[t_ss])
            P.emit("dve", I("scalar_tensor_tensor", out=hb, in0=xsrc, scalar=lnv, in1=gbv_, op0=ALU.mult, op1=ALU.mult),
                   reads=[t_x, t_ss, t_gb], writes=[t_hb])

        def norm_B(hb, t_hb, hT_view, t_hT, evac_i):
            bi = nb()
            bkb = bank(bi).bitcast(BF16)
            for kc in range(8):
                P.emit("pe", I("transpose", out=bkb[:, kc * 128:(kc + 1) * 128],
                               in_=hb[:, kc * 128:(kc + 1) * 128], identity=ident),
                       reads=[t_hb, t_const], writes=[tb[bi]])
            src = bkb.rearrange("p (k t) -> p k t", t=128)
            P.emit("dve", I("tensor_copy", out=hT_view, in_=src), reads=[tb[bi]], writes=[t_hT])

        def sched_AB(Apieces, NA, NB_, a_at, b_at):
            for i, a in enumerate(Apieces):
                a()
                for k in range(len(NA)):
                    if b_at[k] == i:
                        NB_[k]()
                for k in range(len(NA)):
                    if a_at[k] == i:
                        NA[k]()
            last = len(Apieces) - 1
            for k in range(len(NA)):
                if a_at[k] > last:
                    NA[k]()
                if b_at[k] > last:
                    NB_[k]()

        t_pc = [Tok(f"pc{i}") for i in range(4)]

        def precast(dst, src, r0, r1, c0, c1, sc0, grp):
            P.emit("pool", I("dma_start", out=dst[r0:r1, c0:c1], in_=src[r0:r1, sc0:sc0 + (c1 - c0)]),
                   writes=[t_pc[grp]], stream=f"pc{grp}")

        A1 = Alloc([(COMMON_END, ARENA_BYTES)])
        o_w = A1.get(8 * 1536 * 2)
        o_g1b = A1.get(4096)
        o_xs = [A1.get(4096) for _ in range(2)]
        o_hb = [A1.get(2048) for _ in range(2)]
        o_hT = [A1.get(8192) for _ in range(2)]
        o_sq = [A1.get(1024) for _ in range(2)]
        o_lnr = [A1.get(2048) for _ in range(2)]
        wkvq = view(o_w, [8, 1536], BF16)
        g1bv = view(o_g1b, [1024], F32)
        xs = [view(o, [1024], F32) for o in o_xs]
        hbs = [view(o, [1024], BF16) for o in o_hb]
        hTs = [view(o, [8, 512], BF16) for o in o_hT]
        sqs = [view(o, [512], BF16) for o in o_sq]
        lnr = [view(o, [512], F32) for o in o_lnr]
        t_w = Tok("wkvq"); t_g1b = Tok("g1b")
        t_xs = [Tok("xs") for _ in range(2)]
        t_hb = [Tok("hb") for _ in range(2)]
        t_hT = [Tok("hT") for _ in range(2)]
        t_sq = [Tok("sq") for _ in range(2)]
        t_lnr = [Tok("lnr") for _ in range(2)]

        P.emit("sync", I("dma_start", out=g1bv, in_=g1b), writes=[t_g1b], stream="ld_c")
        t_wp = {0: Tok("wk"), 512: Tok("wv"), 1024: Tok("wq")}
        for (dst, src, stn) in ((0, 1024, "ld_w"), (1024, 512, "ld_wq"), (512, 1536, "ld_wv")):
            for kc in range(8):
                P.emit("pool", I("dma_start", out=wkvq[:, kc, dst:dst + 512], in_=w_in[kc * 128:(kc + 1) * 128, src:src + 512]),
                       writes=[t_wp[dst]], stream=stn)

        if PRECAST:
            for r in range(0, 1024, 256):
                precast(wo_b, w_o, r, r + 256, 0, 1024, 0, 0)
            for r in range(0, 512, 256):
                precast(wa_b, w_a, r, r + 256, 0, 1024, 0, 0)
                precast(wb_b, w_b, r, r + 256, 0, 1024, 0, 0)
            for r in range(0, 1024, 256):
                precast(wg_b, w_in, r, r + 256, 0, 2048, 2048, 1)
            for r in range(0, 1024, 256):
                precast(wu_b, w_in, r, r + 256, 0, 512, 0, 2)
            for r in range(0, 512, 256):
                precast(wp_b, pool_w, r, r + 256, 0, 128, 0, 2)
            for r in range(0, 1024, 256):
                for c in range(0, 4096, 2048):
                    precast(w1_b, w_1, r, r + 256, c, c + 2048, c, 3)
            for r in range(0, 4096, 256):
                precast(w2_b, w_2, r, r + 256, 0, 1024, 0, 3)


        lamv = view(o_hT[1], [256], F32)
        t_lam = Tok("lam")
        P.emit("sync", I("dma_start", out=lamv, in_=lam), writes=[t_lam], stream="ld_c")
        P.emit("dve", I("scalar_tensor_tensor", out=lamv[:, 0:64], in0=lamv[:, 0:64], scalar=1.0, in1=lamv[:, 64:128],
                        op0=ALU.mult, op1=ALU.mult, accum_out=small[:, 4:5]),
               reads=[t_lam, t_small], writes=[t_lam, t_small])
        P.emit("dve", I("scalar_tensor_tensor", out=lamv[:, 128:192], in0=lamv[:, 128:192], scalar=1.0, in1=lamv[:, 192:256],
                        op0=ALU.mult, op1=ALU.mult, accum_out=small[:, 5:6]),
               reads=[t_lam, t_small], writes=[t_lam, t_small])
        P.emit("act", I("activation", out=small[:, 4:6], in_=small[:, 4:6], func=AF.Exp), reads=[t_small], writes=[t_small])
        P.emit("dve", I("tensor_tensor", out=small[:, 3:4], in0=small[:, 5:6], in1=small[:, 4:5], op=ALU.subtract),
               reads=[t_small], writes=[t_small])
        P.emit("dve", I("tensor_scalar", out=small[:, 3:4], in0=small[:, 3:4], scalar1=-LAMBDA_INIT, scalar2=None, op0=ALU.add),
               reads=[t_small], writes=[t_small])
        P.emit("dve", I("tensor_scalar", out=small[:, 2:3], in0=small[:, 2:3], scalar1=1.0 - LAMBDA_INIT, scalar2=None, op0=ALU.mult),
               reads=[t_small], writes=[t_small])

        blkctr = [0]
        qk_ctr = [0]

        def norm_pieces(pos):
            hsl = pos % 2
            NA, NB_ = [], []
            for blk in range(4):
                st = {}

                def pa(blk=blk, st=st):
                    sl = blkctr[0] % 2
                    blkctr[0] += 1
                    st["sl"] = sl
                    st["ev"] = blkctr[0]
                    r0 = pos * 512 + blk * 128
                    P.emit("sync", I("dma_start", out=xs[sl], in_=xkv[r0:r0 + 128, :]),
                           writes=[t_xs[sl]], stream="ld_xa" if sl == 0 else "ld_xb")
                    norm_A(xs[sl], t_xs[sl], hbs[sl], t_hb[sl], g1bv, t_g1b)

                def pb(blk=blk, st=st):
                    sl = st["sl"]
                    norm_B(hbs[sl], t_hb[sl], hTs[hsl][:, :, blk * 128:(blk + 1) * 128], t_hT[hsl], st["ev"])
                NA.append(pa)
                NB_.append(pb)
            return NA, NB_

        def proj_pieces(pos):
            hsl = pos % 2
            hT = hTs[hsl]
            th = t_hT[hsl]
            pcs = []
            pend = [None]

            def qk_head(wcol0, gcol, dstv, t_dst, h):
                def piece():
                    bi = nb()
                    for kc in range(8):
                        P.emit("pe", I("matmul", bank(bi), lhsT=wkvq[:, kc, wcol0 + h * 128:wcol0 + (h + 1) * 128],
                                       rhs=hT[:, kc, :], start=(kc == 0), stop=(kc == 7)),
                               reads=[t_wp[wcol0], th], writes=[tb[bi]])
                    sl = qk_ctr[0] % 2
                    qk_ctr[0] += 1
                    P.emit("act", I("activation", out=sqs[sl], in_=bank(bi), func=AF.Square),
                           reads=[tb[bi]], writes=[t_sq[sl]])
                    if len(pend) > 1:
                        pend.pop(1)()

                    def fin():
                        bj = nb()
                        P.emit("pe", I("matmul", bank(bj), lhsT=blk64, rhs=sqs[sl], start=True, stop=True),
                               reads=[t_const, t_sq[sl]], writes=[tb[bj]])
                        P.emit("act", I("activation", out=lnr[sl], in_=bank(bj), func=AF.Ln, scale=1.0 / 64, bias=EPS),
                               reads=[tb[bj]], writes=[t_lnr[sl]])
                        P.emit("act", I("activation", out=lnr[sl], in_=lnr[sl], func=AF.Exp, scale=-0.5),
                               reads=[t_lnr[sl]], writes=[t_lnr[sl]])
                        P.emit("dve", I("scalar_tensor_tensor", out=dstv, in0=bank(bi),
                                        scalar=small[:, gcol:gcol + 1], in1=lnr[sl], op0=ALU.mult, op1=ALU.mult),
                               reads=[tb[bi], t_lnr[sl], t_small], writes=[t_dst])
                    pend.append(fin)
                return piece

            def v_blk(blk):
                def piece():
                    if len(pend) > 1:
                        pend.pop(1)()
                    bi = nb()
                    for kc in range(8):
                        P.emit("pe", I("matmul", bank(bi), lhsT=hT[:, kc, blk * 128:(blk + 1) * 128],
                                       rhs=wkvq[:, kc, 512:1024], start=(kc == 0), stop=(kc == 7)),
                               reads=[t_wp[512], th], writes=[tb[bi]])
                    P.emit("dve", I("tensor_copy", out=Vp[pos][:, blk, :], in_=bank(bi)), reads=[tb[bi]], writes=[t_V[pos]])
                return piece

            for h in range(4):
                pcs.append(qk_head(0, 1, KTp[pos][:, h, :], t_KT[pos], h))
            if pos < 4:
                for h in range(4):
                    pcs.append(qk_head(1024, 0, QT[:, h, pos * 512:(pos + 1) * 512], t_QT[pos], h))
            for blk in range(4):
                pcs.append(v_blk(blk))
            return pcs

        NA, NB_ = norm_pieces(0)
        NA[0](); NA[1](); NB_[0](); NA[2](); NB_[1](); NA[3](); NB_[2](); NB_[3]()
        for pos in range(NPOS):
            A = proj_pieces(pos)
            if pos + 1 < NPOS:
                NA, NB_ = norm_pieces(pos + 1)
                n = len(A)
                if n == 8:
                    a_at = [0, 1, 2, 4]
                    b_at = [2, 4, 6, 7]
                else:
                    a_at = [0, 2, 4, 7]
                    b_at = [3, 6, 9, 11]
                sched_AB(A, NA, NB_, a_at, b_at)
            else:
                for a in A:
                    a()

        P.barrier()
        A2 = Alloc([(COMMON_END, ARENA_BYTES)])
        o_onT = A2.get(4 * 2048 * 2)
        ONT_END = A2.get(0)
        o_bias = A2.get(NBC * 4)
        o_tri = A2.get(256); o_ones = A2.get(256); o_onesf = A2.get(512)
        NPT = 4
        o_PT = [A2.get(2048) for _ in range(NPT)]
        o_f = [A2.get(2048) for _ in range(7)]
        onT = view(o_onT, [4, 2048], BF16)
        biasv = view(o_bias, [NBC], F32)
        tri = view(o_tri, [128], BF16); ones = view(o_ones, [128], BF16); onesf = view(o_onesf, [128], F32)
        PT = [view(o, [1024], BF16) for o in o_PT]
        fo1, ft2, fz1, fz2, fo, fsq, frs = [view(o, [512], F32) for o in o_f]
        t_onT = [Tok(f"onT{s}") for s in range(4)]
        t_bias = Tok("bias"); t_const2 = Tok("const2")
        t_PT = [Tok("PT") for _ in range(NPT)]
        t_f = [Tok("f") for _ in range(7)]
        P.emit("sync", I("dma_start", out=biasv, in_=biastab), writes=[t_bias], stream="ld_c")
        P.emit("pool", I("dma_start", out=tri, in_=cst[:, 128:256]), writes=[t_const2], stream="ld_cp")
        P.emit("dve", I("memset", ones, 1.0), writes=[t_const2])
        P.emit("dve", I("memset", onesf, 1.0), writes=[t_const2])

        def kvm(m):
            return KV0 + m * 8192
        o_wo = kvm(12); o_wa = kvm(14); o_wb = kvm(15)
        o_wg = kvm(8)
        o_wu = kvm(4); o_wp = kvm(5); o_g1b3 = kvm(5) + 1024; o_gb = kvm(5) + 1024 + 4096; o_corr = o_gb + 64
        M47_FREE = o_corr + 256
        wo = view(o_wo, [8, 1024], BF16); wa = view(o_wa, [4, 1024], BF16); wb = view(o_wb, [4, 1024], BF16)
        wg = view(o_wg, [8, 2048], BF16)
        wu = view(o_wu, [8, 512], BF16); wp = view(o_wp, [4, 128], BF16)
        g1b3 = view(o_g1b3, [1024], F32); gbv = view(o_gb, [16], F32); corrv = view(o_corr, [4, 16], F32)
        t_w3 = [Tok("w3a"), Tok("w3b"), Tok("w3c")]
        t_c3 = Tok("c3")

        def kv_toks(ms):
            ts = []
            for m in ms:
                ts += [t_KT[MEMORD[m]], t_V[MEMORD[m]]]
            return ts

        def load_p3a_group(grp):
            q = "sync" if PRECAST else "pool"
            if grp == 0:
                wr = kv_toks([12, 13, 14, 15]) + [t_w3[0]]
                srcs = [(wo, wo_b if PRECAST else w_o, 8), (wa, wa_b if PRECAST else w_a, 4), (wb, wb_b if PRECAST else w_b, 4)]
                rd = [t_pc[0]]
            elif grp == 1:
                wr = kv_toks([8, 9, 10, 11]) + [t_w3[1]]
                srcs = [(wg, wg_b if PRECAST else w_in[:, 2048:4096], 8)]
                rd = [t_pc[1]]
            else:
                wr = kv_toks([4, 5, 6, 7]) + [t_w3[2], t_c3]
                srcs = [(wu, wu_b if PRECAST else w_in[:, 0:512], 8), (wp, wp_b if PRECAST else pool_w, 4)]
                rd = [t_pc[2]]
            for (dstv, srcap, nk) in srcs:
                for kc in range(nk):
                    P.emit(q, I("dma_start", out=dstv[:, kc, :], in_=srcap[kc * 128:(kc + 1) * 128, :]),
                           reads=rd, writes=wr, stream=f"lw{grp}")
            if grp == 2:
                P.emit("sync", I("dma_start", out=g1b3, in_=g1b), writes=wr, stream="lw2")
                P.emit("sync", I("dma_start", out=gbv, in_=gate_b), writes=wr, stream="lw2")
                P.emit("sync", I("dma_start", out=corrv, in_=corr), writes=wr, stream="lw2")

        tiles = []
        for s in (3, 2, 1, 0):
            units = slot_units(s)
            for h in range(4):
                lst = []
                for u, (kind, pos) in enumerate(units):
                    for t in range(4):
                        lst.append(dict(s=s, h=h, u=u, pos=pos, t=t, dg=(kind == "diag")))
                lst[0]["first"] = True
                lst[-1]["last"] = True
                tiles += lst
        NT = len(tiles)
        ptc = [0]
        sc = [0]

        def emit_qk(tl):
            s, h, u, pos, t, dg = tl["s"], tl["h"], tl["u"], tl["pos"], tl["t"], tl["dg"]
            q0 = s * 512
            c0 = 128 * t if dg else 0
            sp = sc[0] % 2
            sc[0] += 1
            for comp in range(2):
                pr = slice(64 * comp, 64 * comp + 64)
                bi = 2 * sp + comp
                P.emit("pe", I("matmul", bank(bi)[:, c0:512], lhsT=KTp[pos][pr, h, t * 128:(t + 1) * 128],
                               rhs=QT[pr, h, q0 + c0:q0 + 512], start=True, stop=True),
                       reads=[t_KT[pos], t_QT[s]], writes=[tb[bi]])
            pt = ptc[0] % NPT
            ptc[0] += 1
            spair = psum[:, 1024 * sp:1024 * (sp + 1)].rearrange("p (c q) -> p c q", c=2)
            ptv = PT[pt].rearrange("p (c q) -> p c q", c=2)
            if h == 0:
                for jj in range(2):
                    lo, hi = max(c0, 256 * jj), 256 * (jj + 1)
                    if lo >= hi:
                        continue
                    bc = bcol(s, u, t, 0, jj)
                    P.emit("act", I("activation", out=ptv[:, :, lo:hi], in_=spair[:, :, lo:hi],
                                    func=AF.Exp, scale=SCALE, bias=biasv[:, bc:bc + 1]),
                           reads=[tb[2 * sp], tb[2 * sp + 1], t_bias], writes=[t_PT[pt]])
            else:
                bc = bcol(s, u, t, h)
                P.emit("act", I("activation", out=ptv[:, :, c0:512], in_=spair[:, :, c0:512],
                                func=AF.Exp, scale=SCALE, bias=biasv[:, bc:bc + 1]),
                       reads=[tb[2 * sp], tb[2 * sp + 1], t_bias], writes=[t_PT[pt]])
            if dg:
                for comp in range(2):
                    cc = 512 * comp + c0
                    P.emit("pool", I("tensor_tensor", out=PT[pt][:, cc:cc + 128], in0=PT[pt][:, cc:cc + 128], in1=tri, op=ALU.mult),
                           reads=[t_PT[pt], t_const2], writes=[t_PT[pt]])
            tl["pt"] = pt
            tl["c0"] = c0

        def emit_pv(tl):
            pos, t, h = tl["pos"], tl["t"], tl["h"]
            pt, c0 = tl["pt"], tl["c0"]
            first = tl.get("first", False)
            last = tl.get("last", False)
            for comp in range(2):
                rhs = PT[pt][:, 512 * comp + c0:512 * comp + 512]
                P.emit("pe", I("matmul", bank(4 + comp)[:, c0:512], lhsT=Vp[pos][:, t, h * 128:(h + 1) * 128], rhs=rhs,
                               start=first, stop=last), reads=[t_V[pos], t_PT[pt]], writes=[tb[4 + comp]])
            for comp in range(2):
                rhs = PT[pt][:, 512 * comp + c0:512 * comp + 512]
                P.emit("pe", I("matmul", bank(6 + comp)[:, c0:512], lhsT=ones, rhs=rhs,
                               start=first, stop=last), reads=[t_const2, t_PT[pt]], writes=[tb[6 + comp]])

        def finalize_part1():
            P.emit("act", I("activation", out=fo1, in_=bank(4), func=AF.Copy), reads=[tb[4]], writes=[t_f[0]])
            P.emit("dve", I("tensor_copy", out=fz1, in_=bank(6)), reads=[tb[6]], writes=[t_f[2]])
            P.emit("act", I("activation", out=ft2, in_=bank(5), func=AF.Copy), reads=[tb[5]], writes=[t_f[1]])
            P.emit("dve", I("tensor_copy", out=fz2, in_=bank(7)), reads=[tb[7]], writes=[t_f[3]])
            P.emit("dve", I("reciprocal", out=fz1, in_=fz1), reads=[t_f[2]], writes=[t_f[2]])
            P.emit("dve", I("tensor_tensor", out=fo1, in0=fo1, in1=fz1, op=ALU.mult), reads=[t_f[2]], writes=[t_f[0]])
            P.emit("dve", I("reciprocal", out=fz2, in_=fz2), reads=[t_f[3]], writes=[t_f[3]])
            P.emit("dve", I("tensor_tensor", out=ft2, in0=ft2, in1=fz2, op=ALU.mult), reads=[t_f[3]], writes=[t_f[1]])
            P.emit("dve", I("scalar_tensor_tensor", out=fo, in0=ft2, scalar=small[:, 3:4], in1=fo1, op0=ALU.mult, op1=ALU.add),
                   reads=[t_f[0], t_f[1], t_small], writes=[t_f[4]])
            P.emit("pool", I("tensor_tensor", out=fsq, in0=fo, in1=fo, op=ALU.mult), reads=[t_f[4]], writes=[t_f[5]])

        def finalize_part2(s, h):
            sp = sc[0] % 2
            sc[0] += 1
            bi = 2 * sp
            P.emit("pe", I("matmul", bank(bi), lhsT=onesf, rhs=fsq, start=True, stop=True),
                   reads=[t_const2, t_f[5]], writes=[tb[bi]])
            P.emit("act", I("activation", out=frs, in_=bank(bi), func=AF.Ln, scale=1.0 / 128, bias=EPS),
                   reads=[tb[bi]], writes=[t_f[6]])
            P.emit("act", I("activation", out=frs, in_=frs, func=AF.Exp, scale=-0.5), reads=[t_f[6]], writes=[t_f[6]])
            P.emit("dve", I("scalar_tensor_tensor", out=onT[:, h, s * 512:(s + 1) * 512], in0=fo, scalar=small[:, 2:3],
                            in1=frs, op0=ALU.mult, op1=ALU.mult),
                   reads=[t_f[4], t_f[6], t_small], writes=[t_onT[s]])

        LAG = 2
        FDELAY = 3
        qp = 0
        sched = []
        for _ in range(LAG):
            if qp < NT:
                emit_qk(tiles[qp]); qp += 1
        for i in range(NT):
            if qp < NT:
                emit_qk(tiles[qp]); qp += 1
            tl = tiles[i]
            emit_pv(tl)
            if tl.get("last", False):
                finalize_part1()
                sched.append((i + FDELAY, tl["s"], tl["h"]))
                if tl["h"] == 3 and tl["s"] > 0:
                    load_p3a_group(3 - tl["s"])
            while sched and sched[0][0] <= i:
                _, s_, h_ = sched.pop(0)
                finalize_part2(s_, h_)
        while sched:
            _, s_, h_ = sched.pop(0)
            finalize_part2(s_, h_)

        def alias(tok, olds):
            for o in olds:
                for ins in ([o.w] if o.w is not None else []) + list(o.r.values()):
                    cur = tok.r.get(ins.stream)
                    if cur is None or cur.idx < ins.idx:
                        tok.r[ins.stream] = ins
            return tok

        P2_TMP = [t_bias, t_const2] + t_PT + t_f
        KV03 = kv_toks([0, 1, 2, 3])
        KV47 = kv_toks([4, 5, 6, 7])
        A3 = Alloc([(kvm(0), kvm(4)), (M47_FREE, kvm(8)), (KV_END, COMMON_END), (ONT_END, ARENA_BYTES)])
        o_x4 = [A3.get(4 * 1024 * 4) for _ in range(2)]
        o_hT3 = [A3.get(8 * 512 * 2) for _ in range(2)]
        o_merged = A3.get(8 * 512 * 2)
        o_pooled = A3.get(4 * 512 * 2)
        o_mixed = A3.get(4 * 512 * 2)
        o_hTh = [A3.get(8 * 128 * 2) for _ in range(2)]
        o_xh = A3.get(4096)
        o_hb3 = [A3.get(2048) for _ in range(2)]
        o_uT = A3.get(4 * 528 * 4)
        o_s2 = A3.get(4 * 528 * 4)
        o_s3 = A3.get(4 * 528 * 4)
        o_gt = [A3.get(2048) for _ in range(4)]
        o_tmpm = [A3.get(2048) for _ in range(2)]
        x4s = [view(o, [4, 1024], F32) for o in o_x4]
        hT3s = [view(o, [8, 512], BF16) for o in o_hT3]
        hThs = [view(o, [8, 128], BF16) for o in o_hTh]
        xh = view(o_xh, [1024], F32)
        hb3 = [view(o, [1024], BF16) for o in o_hb3]
        uT = view(o_uT, [4, 528], F32); s2 = view(o_s2, [4, 528], F32); s3 = view(o_s3, [4, 528], F32)
        pooled = view(o_pooled, [4, 512], BF16); mixed = view(o_mixed, [4, 512], BF16)
        gts = [view(o, [512], F32) for o in o_gt]
        tmpm = [view(o, [512], F32) for o in o_tmpm]
        merged = view(o_merged, [8, 512], BF16)
        t_x4 = [[alias(Tok("x4"), KV03) for _ in range(4)] for _ in range(2)]
        t_xh = alias(Tok("xh"), P2_TMP)
        t_hb3 = [alias(Tok("hb3"), P2_TMP) for _ in range(2)]
        t_hT3 = [alias(Tok("hT3"), KV47) for _ in range(2)]
        t_hTh = [alias(Tok("hTh"), KV47 + P2_TMP) for _ in range(2)]
        t_uT = alias(Tok("uT"), P2_TMP); t_s2 = alias(Tok("s2"), P2_TMP); t_s3 = alias(Tok("s3"), P2_TMP)
        t_pooled = alias(Tok("pooled"), t_QT); t_mixed = alias(Tok("mixed"), t_QT)
        t_gt = [alias(Tok("gt"), P2_TMP) for _ in range(4)]; t_tmpm = [alias(Tok("tmpm"), P2_TMP) for _ in range(2)]
        t_merged = alias(Tok("merged"), t_QT)
        t_x1d = [[Tok("x1d") for _ in range(4)] for _ in range(4)]
        if not PRECAST:
            pass
        hbc = [0]

        def p3a_norm_pieces(s):
            d = s % 2
            NA, NB_ = [], []
            st0 = {}

            def loads():
                for blk in range(4):
                    r0 = s * 512 + blk * 128
                    P.emit("sync", I("dma_start", out=x4s[d][:, blk, :], in_=xkv[r0:r0 + 128, :]),
                           writes=[t_x4[d][blk]], stream="ld_xa")

            def halo_a():
                P.emit("sync", I("dma_start", out=xh, in_=xhalo[s * 128:(s + 1) * 128, :]), writes=[t_xh], stream="ld_h")
                k = hbc[0] % 2
                hbc[0] += 1
                st0["k"] = k
                norm_A(xh, t_xh, hb3[k], t_hb3[k], g1b3, t_c3)

            def halo_b():
                k = st0["k"]
                norm_B(hb3[k], t_hb3[k], hThs[d][:, :, :], t_hTh[d], 0)
            NA.append(halo_a)
            NB_.append(halo_b)
            for blk in range(4):
                st = {}

                def pa(blk=blk, st=st):
                    k = hbc[0] % 2
                    hbc[0] += 1
                    st["k"] = k
                    norm_A(x4s[d][:, blk, :], t_x4[d][blk], hb3[k], t_hb3[k], g1b3, t_c3)

                def pb(blk=blk, st=st):
                    k = st["k"]
                    norm_B(hb3[k], t_hb3[k], hT3s[d][:, :, blk * 128:(blk + 1) * 128], t_hT3[d], blk + 1)
                NA.append(pa)
                NB_.append(pb)
            return NA, NB_, loads

        def p3a_comp_pieces(s):
            d = s % 2
            hT3 = hT3s[d]; hTh = hThs[d]; x4 = x4s[d]
            th3 = t_hT3[d]; thh = t_hTh[d]
            pcs = []

            def pool1():
                for g in range(4):
                    bi = nb()
                    for kc in range(8):
                        P.emit("pe", I("matmul", bank(bi)[:, 0:128], lhsT=wu[:, kc, g * 128:(g + 1) * 128],
                                       rhs=hTh[:, kc, :], start=(kc == 0), stop=(kc == 7)),
                               reads=[t_w3[2], thh], writes=[tb[bi]])
                    P.emit("act", I("activation", out=uT[:, g, 0:16], in_=bank(bi)[:, 112:128], func=AF.Copy),
                           reads=[tb[bi]], writes=[t_uT])
                    bi2 = nb()
                    for kc in range(8):
                        P.emit("pe", I("matmul", bank(bi2), lhsT=wu[:, kc, g * 128:(g + 1) * 128],
                                       rhs=hT3[:, kc, :], start=(kc == 0), stop=(kc == 7)),
                               reads=[t_w3[2], th3], writes=[tb[bi2]])
                    P.emit("dve", I("tensor_copy", out=uT[:, g, 16:528], in_=bank(bi2)), reads=[tb[bi2]], writes=[t_uT])
                for g in range(4):
                    cur, tcur = uT, t_uT
                    k = 1
                    bufs = [(s2, t_s2), (s3, t_s3)]
                    bi_ = 0
                    while k < POOL_W[g]:
                        dstb, tdst = bufs[bi_ % 2]
                        bi_ += 1
                        eng = "pool" if g % 2 == 0 else "dve"
                        P.emit(eng, I("tensor_tensor", out=dstb[:, g, k:528], in0=cur[:, g, k:528], in1=cur[:, g, 0:528 - k], op=ALU.add),
                               reads=[tcur], writes=[tdst])
                        cur, tcur = dstb, tdst
                        k *= 2
                    if s == 0:
                        P.emit("dve", I("tensor_tensor", out=cur[:, g, 16:32], in0=cur[:, g, 16:32], in1=corrv[:, g, :], op=ALU.mult),
                               reads=[tcur, t_c3], writes=[tcur])
                    P.emit("dve", I("scalar_tensor_tensor", out=pooled[:, g, :], in0=cur[:, g, 16:528], scalar=1.0 / POOL_W[g],
                                    in1=uT[:, g, 16:528], op0=ALU.mult, op1=ALU.subtract),
                           reads=[tcur, t_uT], writes=[t_pooled])

            def pool2():
                for g in range(4):
                    bi = nb()
                    P.emit("pe", I("matmul", bank(bi), lhsT=wp[:, g, :], rhs=pooled[:, g, :], start=True, stop=True),
                           reads=[t_w3[2], t_pooled], writes=[tb[bi]])
                    P.emit("act", I("activation", out=mixed[:, g, :], in_=bank(bi), func=AF.Copy, scale=small[:, 24 + g:25 + g]),
                           reads=[tb[bi], t_small], writes=[t_mixed])

            gate_banks = {}

            def gates_piece(fo_):
                def piece():
                    for br in range(2):
                        bg = nb()
                        c0 = br * 1024 + fo_ * 128
                        for kc in range(8):
                            P.emit("pe", I("matmul", bank(bg), lhsT=wg[:, kc, c0:c0 + 128], rhs=hT3[:, kc, :],
                                           start=(kc == 0), stop=(kc == 7)),
                                   reads=[t_w3[1], th3], writes=[tb[bg]])
                        gi = (fo_ % 2) * 2 + br
                        P.emit("act", I("activation", out=gts[gi], in_=bank(bg), func=AF.Sigmoid,
                                        bias=gbv[:, br * 8 + fo_:br * 8 + fo_ + 1]),
                               reads=[tb[bg], t_c3], writes=[t_gt[gi]])
                return piece

            def br_piece(fo_):
                def piece():
                    b_ya, b_yb = nb(), nb()
                    for g in range(4):
                        P.emit("pe", I("matmul", bank(b_ya), lhsT=wa[:, g, fo_ * 128:(fo_ + 1) * 128],
                                       rhs=mixed[:, g, :], start=(g == 0), stop=(g == 3)),
                               reads=[t_w3[0], t_mixed], writes=[tb[b_ya]])
                    for h in range(4):
                        P.emit("pe", I("matmul", bank(b_yb), lhsT=wb[:, h, fo_ * 128:(fo_ + 1) * 128],
                                       rhs=onT[:, h, s * 512:(s + 1) * 512], start=(h == 0), stop=(h == 3)),
                               reads=[t_w3[0], t_onT[s]], writes=[tb[b_yb]])
                    tm = fo_ % 2
                    g0i = (fo_ % 2) * 2
                    P.emit("dve", I("tensor_tensor", out=tmpm[tm], in0=bank(b_ya), in1=gts[g0i], op=ALU.mult),
                           reads=[tb[b_ya], t_gt[g0i]], writes=[t_tmpm[tm]])
                    P.emit("dve", I("tensor_tensor", out=gts[g0i + 1], in0=bank(b_yb), in1=gts[g0i + 1], op=ALU.mult),
                           reads=[tb[b_yb], t_gt[g0i + 1]], writes=[t_gt[g0i + 1]])
                    P.emit("pool", I("tensor_tensor", out=merged[:, fo_, :], in0=tmpm[tm], in1=gts[g0i + 1], op=ALU.add),
                           reads=[t_tmpm[tm], t_gt[g0i + 1]], writes=[t_merged])
                return piece

            pcs += [pool1, gates_piece(0), gates_piece(1), pool2]
            for fo_ in range(6):
                pcs += [br_piece(fo_), gates_piece(fo_ + 2)]
            pcs += [br_piece(6), br_piece(7)]

            def out_piece(blk):
                def piece():
                    for half in range(2):
                        bi = nb()
                        for kc in range(8):
                            P.emit("pe", I("matmul", bank(bi), lhsT=merged[:, kc, blk * 128:(blk + 1) * 128],
                                           rhs=wo[:, kc, half * 512:(half + 1) * 512], start=(kc == 0), stop=(kc == 7)),
                                   reads=[t_w3[0], t_merged], writes=[tb[bi]])
                        P.emit("dve", I("tensor_tensor", out=x4[:, blk, half * 512:(half + 1) * 512], in0=bank(bi),
                                        in1=x4[:, blk, half * 512:(half + 1) * 512], op=ALU.add),
                               reads=[tb[bi]], writes=[t_x4[d][blk]])
                    r0 = s * 512 + blk * 128
                    P.emit("sync", I("dma_start", out=x1s[r0:r0 + 128, :], in_=x4[:, blk, :]),
                           reads=[t_x4[d][blk]], writes=[t_x1d[s][blk]], stream="st_x1")
                return piece
            for blk in range(4):
                pcs.append(out_piece(blk))
            return pcs

        NA, NB_, lds = p3a_norm_pieces(3)
        lds()
        NA[0](); NA[1](); NB_[0](); NA[2](); NB_[1](); NA[3](); NB_[2](); NA[4](); NB_[3](); NB_[4]()
        for s in (3, 2, 1, 0):
            A = p3a_comp_pieces(s)
            if s > 0:
                NA, NB_, lds = p3a_norm_pieces(s - 1)
                lds()
                sched_AB(A, NA, NB_, [4, 6, 8, 10, 12], [8, 10, 12, 14, 16])
            else:
                for a in A:
                    a()

        P.barrier()
        A4 = Alloc([(CONST_END, ARENA_BYTES)])
        o_w1 = A4.get(8 * 4096 * 2)
        o_w2 = A4.get(32 * 1024 * 2)
        o_g2b = A4.get(4096)
        o_st = [A4.get(4096) for _ in range(2)]
        o_hb4 = [A4.get(2048) for _ in range(2)]
        o_h2T = [A4.get(8 * 512 * 2) for _ in range(2)]
        o_aT = A4.get(32 * 512 * 2)
        o_r = [A4.get(2048) for _ in range(2)]
        w1 = view(o_w1, [8, 4096], BF16); w2 = view(o_w2, [32, 1024], BF16)
        g2bv = view(o_g2b, [1024], F32)
        stg = [view(o, [1024], F32) for o in o_st]
        hb4 = [view(o, [1024], BF16) for o in o_hb4]
        h2Ts = [view(o, [8, 512], BF16) for o in o_h2T]
        aT = view(o_aT, [32, 512], BF16)
        rl = [view(o, [512], F32) for o in o_r]
        t_w1 = [Tok("w1") for _ in range(4)]; t_w2 = [Tok("w2") for _ in range(4)]; t_g2b = Tok("g2b")
        t_st = [Tok("st") for _ in range(2)]
        t_hb4 = [Tok("hb4") for _ in range(2)]
        t_h2T = [Tok("h2T") for _ in range(2)]; t_aT = [Tok("aT") for _ in range(32)]; t_rl = [Tok("rl") for _ in range(2)]
        P.emit("sync", I("dma_start", out=g2bv, in_=g2b), writes=[t_g2b], stream="ld_c")
        wq_ = "sync" if PRECAST else "pool"
        def load_ffn_weights(qs1, qs2):
            for q in qs1:
                if PRECAST:
                    for hk in range(2):
                        P.emit(wq_, I("dma_start", out=w1[:, hk * 4:hk * 4 + 4, q * 1024:(q + 1) * 1024],
                                      in_=w1_b[hk * 512:(hk + 1) * 512, q * 1024:(q + 1) * 1024].rearrange("(k p) c -> p k c", p=128)),
                               reads=[t_pc[3]], writes=[t_w1[q]], stream=f"lf{q}")
                else:
                    for kc in range(8):
                        P.emit(wq_, I("dma_start", out=w1[:, kc, q * 1024:(q + 1) * 1024],
                                      in_=w_1[kc * 128:(kc + 1) * 128, q * 1024:(q + 1) * 1024]),
                               writes=[t_w1[q]], stream=f"lf{q}")
            for q in qs2:
                if PRECAST:
                    for hk in range(2):
                        P.emit(wq_, I("dma_start", out=w2[:, q * 8 + hk * 4:q * 8 + hk * 4 + 4, :],
                                      in_=w2_b[q * 1024 + hk * 512:q * 1024 + (hk + 1) * 512, :].rearrange("(k p) c -> p k c", p=128)),
                               reads=[t_pc[3]], writes=[t_w2[q]], stream=f"lf{4 + q}")
                else:
                    for fc in range(q * 8, (q + 1) * 8):
                        P.emit(wq_, I("dma_start", out=w2[:, fc, :], in_=w_2[fc * 128:(fc + 1) * 128, :]),
                               writes=[t_w2[q]], stream=f"lf{4 + q}")

        stc = [0]
        last_out = [None]

        def p3b_norm_pieces(s):
            d = s % 2
            NA, NB_ = [], []
            for blk in range(4):
                st = {}

                def pa(blk=blk, st=st):
                    k = stc[0] % 2
                    stc[0] += 1
                    st["k"] = k
                    r0 = s * 512 + blk * 128
                    P.emit("sync", I("dma_start", out=stg[k], in_=x1s[r0:r0 + 128, :]),
                           reads=[t_x1d[s][blk]], writes=[t_st[k]], stream="ld_x1a" if k == 0 else "ld_x1b")
                    norm_A(stg[k], t_st[k], hb4[k], t_hb4[k], g2bv, t_g2b)

                def pb(blk=blk, st=st):
                    k = st["k"]
                    norm_B(hb4[k], t_hb4[k], h2Ts[d][:, :, blk * 128:(blk + 1) * 128], t_h2T[d], blk)
                NA.append(pa)
                NB_.append(pb)
            return NA, NB_

        def p3b_comp_pieces(s):
            d = s % 2
            h2T = h2Ts[d]
            pcs = []

            def ffn1(fg):
                def piece():
                    for fc in range(fg * 4, fg * 4 + 4):
                        bi = nb()
                        for kc in range(8):
                            P.emit("pe", I("matmul", bank(bi), lhsT=w1[:, kc, fc * 128:(fc + 1) * 128], rhs=h2T[:, kc, :],
                                           start=(kc == 0), stop=(kc == 7)),
                                   reads=[t_w1[fc // 8], t_h2T[d]], writes=[tb[bi]])
                        ri = fc % 2
                        P.emit("act", I("activation", out=rl[ri], in_=bank(bi), func=AF.Relu),
                               reads=[tb[bi]], writes=[t_rl[ri]])
                        P.emit("dve", I("scalar_tensor_tensor", out=aT[:, fc, :], in0=bank(bi), scalar=0.0, in1=rl[ri],
                                        op0=ALU.max, op1=ALU.mult),
                               reads=[tb[bi], t_rl[ri]], writes=[t_aT[fc]])
                return piece
            for fg in range(8):
                pcs.append(ffn1(fg))

            def ffn2(blk):
                def piece():
                    k = stc[0] % 2
                    stc[0] += 1
                    r0 = s * 512 + blk * 128
                    P.emit("sync", I("dma_start", out=stg[k], in_=x1s[r0:r0 + 128, :]),
                           reads=[t_x1d[s][blk]], writes=[t_st[k]], stream="ld_x1a" if k == 0 else "ld_x1b")
                    for half in range(2):
                        bi = nb()
                        for fc in range(32):
                            P.emit("pe", I("matmul", bank(bi), lhsT=aT[:, fc, blk * 128:(blk + 1) * 128],
                                           rhs=w2[:, fc, half * 512:(half + 1) * 512], start=(fc == 0), stop=(fc == 31)),
                                   reads=[t_w2[fc // 8], t_aT[fc]], writes=[tb[bi]])
                        P.emit("dve", I("tensor_tensor", out=stg[k][:, half * 512:(half + 1) * 512], in0=bank(bi),
                                        in1=stg[k][:, half * 512:(half + 1) * 512], op=ALU.add),
                               reads=[tb[bi]], writes=[t_st[k]])
                    last_out[0] = P.emit("sync", I("dma_start", out=out[r0:r0 + 128, :], in_=stg[k]),
                                         reads=[t_st[k]], stream="st_o")
                return piece
            for blk in range(4):
                pcs.append(ffn2(blk))
            return pcs

        NA, NB_ = p3b_norm_pieces(0)
        NA[0](); NA[1]()
        load_ffn_weights([0], [])
        NB_[0](); NA[2](); NB_[1](); NA[3]()
        load_ffn_weights([1, 2, 3], [0, 1, 2, 3])
        NB_[2](); NB_[3]()
        for s in range(4):
            A = p3b_comp_pieces(s)
            if s + 1 < 4:
                NA, NB_ = p3b_norm_pieces(s + 1)
                sched_AB(A, NA, NB_, [0, 1, 2, 3], [2, 3, 4, 5])
            else:
                for a in A:
                    a()

        run = P.finalize(nc, sems, {"sync": [last_out[0]]})

        @block.sync
        def _(e):
            run("sync")

        @block.scalar
        def _(e):
            run("act")

        @block.gpsimd
        def _(e):
            run("pool")

        @block.vector
        def _(e):
            run("dve")

        @block.tensor
        def _(e):
            run("pe")
    return nc


def core_layout(c):
    b, j = c // 4, c % 4
    own = [j, 7 - j, 8 + j, 15 - j]
    others = [x for x in range(16) if x not in own]
    return b, own, own + others


def make_bias(own, pi):
    tab = np.zeros((128, NBC), np.float32)
    p = np.arange(128, dtype=np.float64)
    for s in range(4):
        q0 = 512 * own[s]
        for u, (kind, pos) in enumerate(slot_units(s)):
            chunk = pi[pos]
            real = (chunk < own[s]) if kind != "diag" else True
            for t in range(4):
                kp = 512 * chunk + 128 * t + p
                for h in range(4):
                    if h == 0:
                        for jj in range(2):
                            v = SLOPES[0] * (kp - (q0 + 256 * jj + 128)) if real else NEG
                            tab[:, bcol(s, u, t, 0, jj)] = v if real else NEG
                    else:
                        v = SLOPES[h] * (kp - (q0 + 511)) if real else NEG
                        tab[:, bcol(s, u, t, h)] = np.minimum(v, 0.0) if real else NEG
    return tab


_NC_CACHE = {}


def kernel(x, norm1_g, w_in, gate_b, pool_w, pool_scale, q_norm_g, k_norm_g,
           lambda_q1, lambda_k1, lambda_q2, lambda_k2, subln_g,
           w_branch_a, w_branch_b, w_out, norm2_g, w_ff1, w_ff2):
    f = lambda a: np.ascontiguousarray(np.asarray(a, dtype=np.float32))
    x = f(x)
    if "nc" not in _NC_CACHE:
        _NC_CACHE["nc"] = build_nc()
    nc = _NC_CACHE["nc"]

    def cols(v, n):
        return np.ascontiguousarray(f(v).reshape(n, 128).T)

    ident = np.eye(128, dtype=np.float32)
    tri = (np.arange(128)[None, :] >= np.arange(128)[:, None]).astype(np.float32)
    blk = np.kron(np.eye(2, dtype=np.float32), np.ones((64, 64), np.float32))
    cst = np.ascontiguousarray(np.concatenate([ident, tri, blk], axis=1))
    lamrow = np.concatenate([f(lambda_q1)[0], f(lambda_k1)[0], f(lambda_q2)[0], f(lambda_k2)[0]])
    shared = {
        "w_in": f(w_in)[0], "gate_b": cols(f(gate_b)[0], 16), "pool_w": np.ascontiguousarray(f(pool_w)[0].reshape(512, 128)),
        "pool_scale": cols(f(pool_scale)[0], 4), "g1b": np.ascontiguousarray(np.tile(f(norm1_g)[0][None, :], (128, 1))),
        "g2b": np.ascontiguousarray(np.tile(f(norm2_g)[0][None, :], (128, 1))),
        "gq": np.ascontiguousarray(np.tile(f(q_norm_g)[0], 2)[:, None]),
        "gk": np.ascontiguousarray(np.tile(f(k_norm_g)[0], 2)[:, None]),
        "lam": np.ascontiguousarray(np.tile(lamrow[None, :], (128, 1))),
        "subg": np.ascontiguousarray(f(subln_g)[0][:, None]),
        "w_a": f(w_branch_a)[0], "w_b": f(w_branch_b)[0], "w_o": f(w_out)[0],
        "w_1": f(w_ff1)[0], "w_2": f(w_ff2)[0], "cst": cst,
    }
    in_maps = []
    layouts = []
    for c in range(NCORES):
        b, own, pi = core_layout(c)
        layouts.append((b, own))
        xb = x[b].reshape(16, 512, D)
        xkv = np.ascontiguousarray(xb[pi].reshape(SEQ, D))
        xhalo = np.zeros((4, 128, D), np.float32)
        for s in range(4):
            if own[s] > 0:
                xhalo[s] = x[b, own[s] * 512 - 128:own[s] * 512]
        corr = np.ones((4, 16), np.float32)
        if own[0] == 0:
            for g, w in enumerate(POOL_W):
                for t in range(16):
                    corr[g, t] = w / min(t + 1, w)
        m = dict(shared)
        m["xkv"] = xkv
        m["xhalo"] = np.ascontiguousarray(xhalo.reshape(512, D))
        m["biastab"] = make_bias(own, pi)
        m["corr"] = np.ascontiguousarray(np.tile(corr.reshape(1, 64), (128, 1)))
        in_maps.append(m)
    res = run_bass_kernel_spmd(nc, in_maps, core_ids=list(range(NCORES)))
    outp = np.zeros((2, SEQ, D), np.float32)
    for c in range(NCORES):
        b, own = layouts[c]
        o = np.asarray(res.results[c]["out"]).reshape(4, 512, D)
        for s in range(4):
            outp[b, own[s] * 512:(own[s] + 1) * 512] = o[s]
    return outp
```

```python
import math
from contextlib import ExitStack

import numpy as np
import concourse.bass as bass
import concourse.mybir as mybir
from concourse.bass_utils import run_bass_kernel_spmd
from concourse.alu_op_type import AluOpType as ALU

F32 = mybir.dt.float32
BF16 = mybir.dt.bfloat16
U8 = mybir.dt.uint8
AF = mybir.ActivationFunctionType

D = 1024
SEQ = 8192
CH = 512
NPOS = 16
NCORES = 8
SCALE = 0.125
EPS = 1e-6
SLOPES = [2.0 ** (-8.0 * (h + 1) / 4) for h in range(4)]
LAMBDA_INIT = 0.8 - 0.6 * math.exp(-0.3 * 0)
MAXOTH = [3, 6, 9, 12]
POOL_W = (2, 4, 8, 16)
NEG = -30000.0


def slot_units(s):
    return [("own", t) for t in range(s)] + [("oth", 4 + i) for i in range(MAXOTH[s])] + [("diag", s)]


UNIT_BASE = []
_c = 0
for _s in range(4):
    UNIT_BASE.append(_c)
    _c += len(slot_units(_s))
NUNITS = _c
NBC = NUNITS * 4 * 7


def bcol(s, u, t, h, jj=0):
    return ((UNIT_BASE[s] + u) * 4 + t) * 7 + (jj if h == 0 else 3 + h)


ENGS = ("sync", "act", "pool", "dve", "pe")


class Tok:
    __slots__ = ("name", "w", "r")

    def __init__(self, name=""):
        self.name = name
        self.w = None
        self.r = {}


class Ins:
    __slots__ = ("eng", "fn", "waits", "stream", "idx", "signal")

    def __init__(self, eng, fn, stream):
        self.eng = eng
        self.fn = fn
        self.waits = []
        self.stream = stream
        self.idx = None
        self.signal = False


class Prog:
    def __init__(self):
        self.q = {e: [] for e in ENGS}
        self.streams = {}
        self.waited = {e: {} for e in ENGS}
        self.pending = {e: [] for e in ENGS}

    def barrier(self):
        lasts = [st[-1] for st in self.streams.values() if st]
        for e in ENGS:
            self.pending[e] = list(lasts)

    def emit(self, eng, fn, reads=(), writes=(), stream=None):
        is_dma = stream is not None
        sname = stream if is_dma else eng
        ins = Ins(eng, fn, sname)
        st = self.streams.setdefault(sname, [])
        ins.idx = len(st)
        deps = []
        for t in reads:
            if t.w is not None:
                deps.append(t.w)
        for t in writes:
            if t.w is not None:
                deps.append(t.w)
            deps.extend(t.r.values())
        if self.pending[eng]:
            deps.extend(self.pending[eng])
            self.pending[eng] = []
        wd = self.waited[eng]
        need = {}
        for d in deps:
            if (not is_dma) and eng == "pe" and d.stream == "pe":
                continue
            if d.stream not in ENGS:
                d = self.streams[d.stream][-1]
            if wd.get(d.stream, -1) >= d.idx:
                continue
            if need.get(d.stream) is None or need[d.stream].idx < d.idx:
                need[d.stream] = d
        for sn, d in need.items():
            d.signal = True
            wd[sn] = d.idx
            ins.waits.append(d)
        st.append(ins)
        for t in reads:
            t.r[sname] = ins
        for t in writes:
            t.w = ins
            t.r = {}
        self.q[eng].append(ins)
        return ins

    def finalize(self, nc, sems, final_waits):
        val = {}
        for sname, st in self.streams.items():
            c = 0
            isdma = sname not in ENGS
            for ins in st:
                if isdma:
                    ins.signal = True
                if ins.signal:
                    c += 16 if isdma else 1
                val[id(ins)] = c
        engobj = {"sync": nc.sync, "act": nc.scalar, "pool": nc.gpsimd,
                  "dve": nc.vector, "pe": nc.tensor}

        def run(eng):
            e = engobj[eng]
            for ins in self.q[eng]:
                for d in ins.waits:
                    e.wait_ge(sems[d.stream], val[id(d)])
                r = ins.fn(e)
                if ins.signal:
                    r.then_inc(sems[ins.stream], 16 if ins.stream not in ENGS else 1)
            for d in final_waits.get(eng, ()):
                e.wait_ge(sems[d.stream], val[id(d)])
        return run


def I(name, *a, **k):
    return lambda e: getattr(e, name)(*a, **k)


class Alloc:
    def __init__(self, ranges):
        self.ranges = [[lo, hi] for lo, hi in ranges]

    def get(self, n):
        n = (n + 63) // 64 * 64
        for r in self.ranges:
            if r[1] - r[0] >= n:
                o = r[0]
                r[0] += n
                return o
        raise RuntimeError(f"arena OOM for {n}: {self.ranges}")


ARENA_BYTES = 212736
MEMORD = [0, 4, 5, 6, 1, 7, 8, 9, 2, 10, 11, 12, 3, 13, 14, 15]
PRECAST = True


def build_nc():
    nc = bass.Bass("TRN2", target_bir_lowering=False)

    def din(name, shape):
        return nc.dram_tensor(name, shape, F32, kind="ExternalInput").ap()

    xkv = din("xkv", [SEQ, D])
    xhalo = din("xhalo", [4 * 128, D])
    biastab = din("biastab", [128, NBC])
    corr = din("corr", [128, 64])
    w_in = din("w_in", [D, 4096])
    gate_b = din("gate_b", [128, 16])
    pool_w = din("pool_w", [512, 128])
    pool_scale = din("pool_scale", [128, 4])
    g1b = din("g1b", [128, D])
    g2b = din("g2b", [128, D])
    gq = din("gq", [128, 1])
    gk = din("gk", [128, 1])
    lam = din("lam", [128, 256])
    subg = din("subg", [128, 1])
    w_a = din("w_a", [512, D])
    w_b = din("w_b", [512, D])
    w_o = din("w_o", [D, D])
    w_1 = din("w_1", [D, 4096])
    w_2 = din("w_2", [4096, D])
    cst = din("cst", [128, 3 * 128])
    out = nc.dram_tensor("out", [2048, D], F32, kind="ExternalOutput").ap()
    x1s = nc.dram_tensor("x1s", [2048, D], F32, kind="Internal").ap()

    def dbf(name, shape):
        return nc.dram_tensor(name, shape, BF16, kind="Internal").ap()

    wu_b = dbf("wu_b", [D, 512]); wg_b = dbf("wg_b", [D, 2048]); wa_b = dbf("wa_b", [512, D])
    wb_b = dbf("wb_b", [512, D]); wo_b = dbf("wo_b", [D, D]); wp_b = dbf("wp_b", [512, 128])
    w1_b = dbf("w1_b", [D, 4096]); w2_b = dbf("w2_b", [4096, D])

    P = Prog()
    es = ExitStack()
    with es:
        arena = es.enter_context(nc.sbuf_tensor("arena", [128, ARENA_BYTES], U8))
        psum = es.enter_context(nc.psum_tensor("psum", [128, 4096], F32))
        stream_names = list(ENGS) + ["ld_xa", "ld_xb", "ld_w", "ld_c", "ld_h", "st_o", "st_x1", "ld_x1a", "ld_x1b",
                                     "ld_cp", "ld_wq", "ld_wv", "pc0", "pc1", "pc2", "pc3", "lw0", "lw1", "lw2"] + [f"lf{i}" for i in range(8)]
        sems = {n: es.enter_context(nc.semaphore(n)) for n in stream_names}
        block = es.enter_context(nc.Block())

        def view(off, shape, dt):
            esz = 4 if dt == F32 else 2
            n = 1
            for d_ in shape:
                n *= d_
            v = arena[:, off:off + n * esz].bitcast(dt)
            if len(shape) == 2:
                return v.rearrange("p (a b) -> p a b", b=shape[1])
            if len(shape) == 1:
                return v
            return v.rearrange("p (a b c) -> p a b c", b=shape[1], c=shape[2])

        def bank(i):
            return psum[:, 512 * i:512 * (i + 1)]

        tb = [Tok(f"bank{i}") for i in range(8)]
        rr = [0]

        def nb():
            i = rr[0]
            rr[0] = (i + 1) % 8
            return i

        A0 = Alloc([(0, ARENA_BYTES)])
        o_ident = A0.get(256); o_blk = A0.get(256)
        o_small = A0.get(64 * 4)
        ident = view(o_ident, [128], BF16)
        blk64 = view(o_blk, [128], BF16)
        small = view(o_small, [64], F32)
        t_const = Tok("const")
        t_small = Tok("small")
        for i, o_ in ((0, o_ident), (2, o_blk)):
            P.emit("pool", I("dma_start", out=view(o_, [128], BF16), in_=cst[:, 128 * i:128 * (i + 1)]),
                   writes=[t_const], stream="ld_cp")
        P.emit("sync", I("dma_start", out=small[:, 0:1], in_=gq), writes=[t_small], stream="ld_c")
        P.emit("sync", I("dma_start", out=small[:, 1:2], in_=gk), writes=[t_small], stream="ld_c")
        P.emit("sync", I("dma_start", out=small[:, 2:3], in_=subg), writes=[t_small], stream="ld_c")
        P.emit("sync", I("dma_start", out=small[:, 24:28], in_=pool_scale), writes=[t_small], stream="ld_c")
        CONST_END = A0.get(0)

        KV0 = A0.get(16 * 8192)
        KV_END = A0.get(0)
        o_QT = A0.get(4 * 2048 * 2)
        COMMON_END = A0.get(0)

        def kvbase(pos):
            return KV0 + MEMORD.index(pos) * 8192

        KTp = [view(kvbase(p), [4, 512], BF16) for p in range(NPOS)]
        Vp = [view(kvbase(p) + 4096, [4, 512], BF16) for p in range(NPOS)]
        QT = view(o_QT, [4, 2048], BF16)
        t_KT = [Tok(f"KT{p}") for p in range(NPOS)]
        t_V = [Tok(f"V{p}") for p in range(NPOS)]
        t_QT = [Tok(f"QT{p}") for p in range(4)]

        ssi = [0]

        def norm_A(xsrc, t_x, hb, t_hb, gbv_, t_gb):
            c = ssi[0] % 8
            ssi[0] += 1
            ss = small[:, 8 + c:9 + c]
            lnv = small[:, 16 + c:17 + c]
            t_ss = Tok("ss")
            P.emit("act", I("activation", out=hb, in_=xsrc, func=AF.Square, accum_out=ss),
                   reads=[t_x], writes=[t_hb, t_ss])
            P.emit("act", I("activation", out=lnv, in_=ss, func=AF.Ln, scale=1.0 / D, bias=EPS),
                   reads=[t_ss], writes=[t_ss])
            P.emit("act", I("activation", out=lnv, in_=lnv, func=AF.Exp, scale=-0.5),
                   reads=[t_ss], writes=[t_ss])
            P.emit("dve", I("scalar_tensor_tensor", out=hb, in0=xsrc, scalar=lnv, in1=gbv_, op0=ALU.mult, op1=ALU.mult),
                   reads=[t_x, t_ss, t_gb], writes=[t_hb])

        def norm_B(hb, t_hb, hT_view, t_hT, evac_i):
            bi = nb()
            bkb = bank(bi).bitcast(BF16)
            for kc in range(8):
                P.emit("pe", I("transpose", out=bkb[:, kc * 128:(kc + 1) * 128],
                               in_=hb[:, kc * 128:(kc + 1) * 128], identity=ident),
                       reads=[t_hb, t_const], writes=[tb[bi]])
            src = bkb.rearrange("p (k t) -> p k t", t=128)
            P.emit("dve", I("tensor_copy", out=hT_view, in_=src), reads=[tb[bi]], writes=[t_hT])

        def sched_AB(Apieces, NA, NB_, a_at, b_at):
            for i, a in enumerate(Apieces):
                a()
                for k in range(len(NA)):
                    if b_at[k] == i:
                        NB_[k]()
                for k in range(len(NA)):
                    if a_at[k] == i:
                        NA[k]()
            last = len(Apieces) - 1
            for k in range(len(NA)):
                if a_at[k] > last:
                    NA[k]()
                if b_at[k] > last:
                    NB_[k]()

        t_pc = [Tok(f"pc{i}") for i in range(4)]

        def precast(dst, src, r0, r1, c0, c1, sc0, grp):
            P.emit("pool", I("dma_start", out=dst[r0:r1, c0:c1], in_=src[r0:r1, sc0:sc0 + (c1 - c0)]),
                   writes=[t_pc[grp]], stream=f"pc{grp}")

        A1 = Alloc([(COMMON_END, ARENA_BYTES)])
        o_w = A1.get(8 * 1536 * 2)
        o_g1b = A1.get(4096)
        o_xs = [A1.get(4096) for _ in range(2)]
        o_hb = [A1.get(2048) for _ in range(2)]
        o_hT = [A1.get(8192) for _ in range(2)]
        o_sq = [A1.get(1024) for _ in range(2)]
        o_lnr = [A1.get(2048) for _ in range(2)]
        wkvq = view(o_w, [8, 1536], BF16)
        g1bv = view(o_g1b, [1024], F32)
        xs = [view(o, [1024], F32) for o in o_xs]
        hbs = [view(o, [1024], BF16) for o in o_hb]
        hTs = [view(o, [8, 512], BF16) for o in o_hT]
        sqs = [view(o, [512], BF16) for o in o_sq]
        lnr = [view(o, [512], F32) for o in o_lnr]
        t_w = Tok("wkvq"); t_g1b = Tok("g1b")
        t_xs = [Tok("xs") for _ in range(2)]
        t_hb = [Tok("hb") for _ in range(2)]
        t_hT = [Tok("hT") for _ in range(2)]
        t_sq = [Tok("sq") for _ in range(2)]
        t_lnr = [Tok("lnr") for _ in range(2)]

        P.emit("sync", I("dma_start", out=g1bv, in_=g1b), writes=[t_g1b], stream="ld_c")
        t_wp = {0: Tok("wk"), 512: Tok("wv"), 1024: Tok("wq")}
        for (dst, src, stn) in ((0, 1024, "ld_w"), (1024, 512, "ld_wq"), (512, 1536, "ld_wv")):
            for kc in range(8):
                P.emit("pool", I("dma_start", out=wkvq[:, kc, dst:dst + 512], in_=w_in[kc * 128:(kc + 1) * 128, src:src + 512]),
                       writes=[t_wp[dst]], stream=stn)

        if PRECAST:
            for r in range(0, 1024, 256):
                precast(wo_b, w_o, r, r + 256, 0, 1024, 0, 0)
            for r in range(0, 512, 256):
                precast(wa_b, w_a, r, r + 256, 0, 1024, 0, 0)
                precast(wb_b, w_b, r, r + 256, 0, 1024, 0, 0)
            for r in range(0, 1024, 256):
                precast(wg_b, w_in, r, r + 256, 0, 2048, 2048, 1)
            for r in range(0, 1024, 256):
                precast(wu_b, w_in, r, r + 256, 0, 512, 0, 2)
            for r in range(0, 512, 256):
                precast(wp_b, pool_w, r, r + 256, 0, 128, 0, 2)
            for r in range(0, 1024, 256):
                for c in range(0, 4096, 2048):
                    precast(w1_b, w_1, r, r + 256, c, c + 2048, c, 3)
            for r in range(0, 4096, 256):
                precast(w2_b, w_2, r, r + 256, 0, 1024, 0, 3)


        lamv = view(o_hT[1], [256], F32)
        t_lam = Tok("lam")
        P.emit("sync", I("dma_start", out=lamv, in_=lam), writes=[t_lam], stream="ld_c")
        P.emit("dve", I("scalar_tensor_tensor", out=lamv[:, 0:64], in0=lamv[:, 0:64], scalar=1.0, in1=lamv[:, 64:128],
                        op0=ALU.mult, op1=ALU.mult, accum_out=small[:, 4:5]),
               reads=[t_lam, t_small], writes=[t_lam, t_small])
        P.emit("dve", I("scalar_tensor_tensor", out=lamv[:, 128:192], in0=lamv[:, 128:192], scalar=1.0, in1=lamv[:, 192:256],
                        op0=ALU.mult, op1=ALU.mult, accum_out=small[:, 5:6]),
               reads=[t_lam, t_small], writes=[t_lam, t_small])
        P.emit("act", I("activation", out=small[:, 4:6], in_=small[:, 4:6], func=AF.Exp), reads=[t_small], writes=[t_small])
        P.emit("dve", I("tensor_tensor", out=small[:, 3:4], in0=small[:, 5:6], in1=small[:, 4:5], op=ALU.subtract),
               reads=[t_small], writes=[t_small])
        P.emit("dve", I("tensor_scalar", out=small[:, 3:4], in0=small[:, 3:4], scalar1=-LAMBDA_INIT, scalar2=None, op0=ALU.add),
               reads=[t_small], writes=[t_small])
        P.emit("dve", I("tensor_scalar", out=small[:, 2:3], in0=small[:, 2:3], scalar1=1.0 - LAMBDA_INIT, scalar2=None, op0=ALU.mult),
               reads=[t_small], writes=[t_small])

        blkctr = [0]
        qk_ctr = [0]

        def norm_pieces(pos):
            hsl = pos % 2
            NA, NB_ = [], []
            for blk in range(4):
                st = {}

                def pa(blk=blk, st=st):
                    sl = blkctr[0] % 2
                    blkctr[0] += 1
                    st["sl"] = sl
                    st["ev"] = blkctr[0]
                    r0 = pos * 512 + blk * 128
                    P.emit("sync", I("dma_start", out=xs[sl], in_=xkv[r0:r0 + 128, :]),
                           writes=[t_xs[sl]], stream="ld_xa" if sl == 0 else "ld_xb")
                    norm_A(xs[sl], t_xs[sl], hbs[sl], t_hb[sl], g1bv, t_g1b)

                def pb(blk=blk, st=st):
                    sl = st["sl"]
                    norm_B(hbs[sl], t_hb[sl], hTs[hsl][:, :, blk * 128:(blk + 1) * 128], t_hT[hsl], st["ev"])
                NA.append(pa)
                NB_.append(pb)
            return NA, NB_

        def proj_pieces(pos):
            hsl = pos % 2
            hT = hTs[hsl]
            th = t_hT[hsl]
            pcs = []
            pend = [None]

            def qk_head(wcol0, gcol, dstv, t_dst, h):
                def piece():
                    bi = nb()
                    for kc in range(8):
                        P.emit("pe", I("matmul", bank(bi), lhsT=wkvq[:, kc, wcol0 + h * 128:wcol0 + (h + 1) * 128],
                                       rhs=hT[:, kc, :], start=(kc == 0), stop=(kc == 7)),
                               reads=[t_wp[wcol0], th], writes=[tb[bi]])
                    sl = qk_ctr[0] % 2
                    qk_ctr[0] += 1
                    P.emit("act", I("activation", out=sqs[sl], in_=bank(bi), func=AF.Square),
                           reads=[tb[bi]], writes=[t_sq[sl]])
                    if len(pend) > 1:
                        pend.pop(1)()

                    def fin():
                        bj = nb()
                        P.emit("pe", I("matmul", bank(bj), lhsT=blk64, rhs=sqs[sl], start=True, stop=True),
                               reads=[t_const, t_sq[sl]], writes=[tb[bj]])
                        P.emit("act", I("activation", out=lnr[sl], in_=bank(bj), func=AF.Ln, scale=1.0 / 64, bias=EPS),
                               reads=[tb[bj]], writes=[t_lnr[sl]])
                        P.emit("act", I("activation", out=lnr[sl], in_=lnr[sl], func=AF.Exp, scale=-0.5),
                               reads=[t_lnr[sl]], writes=[t_lnr[sl]])
                        P.emit("dve", I("scalar_tensor_tensor", out=dstv, in0=bank(bi),
                                        scalar=small[:, gcol:gcol + 1], in1=lnr[sl], op0=ALU.mult, op1=ALU.mult),
                               reads=[tb[bi], t_lnr[sl], t_small], writes=[t_dst])
                    pend.append(fin)
                return piece

            def v_blk(blk):
                def piece():
                    if len(pend) > 1:
                        pend.pop(1)()
                    bi = nb()
                    for kc in range(8):
                        P.emit("pe", I("matmul", bank(bi), lhsT=hT[:, kc, blk * 128:(blk + 1) * 128],
                                       rhs=wkvq[:, kc, 512:1024], start=(kc == 0), stop=(kc == 7)),
                               reads=[t_wp[512], th], writes=[tb[bi]])
                    P.emit("dve", I("tensor_copy", out=Vp[pos][:, blk, :], in_=bank(bi)), reads=[tb[bi]], writes=[t_V[pos]])
                return piece

            for h in range(4):
                pcs.append(qk_head(0, 1, KTp[pos][:, h, :], t_KT[pos], h))
            if pos < 4:
                for h in range(4):
                    pcs.append(qk_head(1024, 0, QT[:, h, pos * 512:(pos + 1) * 512], t_QT[pos], h))
            for blk in range(4):
                pcs.append(v_blk(blk))
            return pcs

        NA, NB_ = norm_pieces(0)
        NA[0](); NA[1](); NB_[0](); NA[2](); NB_[1](); NA[3](); NB_[2](); NB_[3]()
        for pos in range(NPOS):
            A = proj_pieces(pos)
            if pos + 1 < NPOS:
                NA, NB_ = norm_pieces(pos + 1)
                n = len(A)
                if n == 8:
                    a_at = [0, 1, 2, 4]
                    b_at = [2, 4, 6, 7]
                else:
                    a_at = [0, 2, 4, 7]
                    b_at = [3, 6, 9, 11]
                sched_AB(A, NA, NB_, a_at, b_at)
            else:
                for a in A:
                    a()

        P.barrier()
        A2 = Alloc([(COMMON_END, ARENA_BYTES)])
        o_onT = A2.get(4 * 2048 * 2)
        ONT_END = A2.get(0)
        o_bias = A2.get(NBC * 4)
        o_tri = A2.get(256); o_ones = A2.get(256); o_onesf = A2.get(512)
        NPT = 4
        o_PT = [A2.get(2048) for _ in range(NPT)]
        o_f = [A2.get(2048) for _ in range(7)]
        onT = view(o_onT, [4, 2048], BF16)
        biasv = view(o_bias, [NBC], F32)
        tri = view(o_tri, [128], BF16); ones = view(o_ones, [128], BF16); onesf = view(o_onesf, [128], F32)
        PT = [view(o, [1024], BF16) for o in o_PT]
        fo1, ft2, fz1, fz2, fo, fsq, frs = [view(o, [512], F32) for o in o_f]
        t_onT = [Tok(f"onT{s}") for s in range(4)]
        t_bias = Tok("bias"); t_const2 = Tok("const2")
        t_PT = [Tok("PT") for _ in range(NPT)]
        t_f = [Tok("f") for _ in range(7)]
        P.emit("sync", I("dma_start", out=biasv, in_=biastab), writes=[t_bias], stream="ld_c")
        P.emit("pool", I("dma_start", out=tri, in_=cst[:, 128:256]), writes=[t_const2], stream="ld_cp")
        P.emit("dve", I("memset", ones, 1.0), writes=[t_const2])
        P.emit("dve", I("memset", onesf, 1.0), writes=[t_const2])

        def kvm(m):
            return KV0 + m * 8192
        o_wo = kvm(12); o_wa = kvm(14); o_wb = kvm(15)
        o_wg = kvm(8)
        o_wu = kvm(4); o_wp = kvm(5); o_g1b3 = kvm(5) + 1024; o_gb = kvm(5) + 1024 + 4096; o_corr = o_gb + 64
        M47_FREE = o_corr + 256
        wo = view(o_wo, [8, 1024], BF16); wa = view(o_wa, [4, 1024], BF16); wb = view(o_wb, [4, 1024], BF16)
        wg = view(o_wg, [8, 2048], BF16)
        wu = view(o_wu, [8, 512], BF16); wp = view(o_wp, [4, 128], BF16)
        g1b3 = view(o_g1b3, [1024], F32); gbv = view(o_gb, [16], F32); corrv = view(o_corr, [4, 16], F32)
        t_w3 = [Tok("w3a"), Tok("w3b"), Tok("w3c")]
        t_c3 = Tok("c3")

        def kv_toks(ms):
            ts = []
            for m in ms:
                ts += [t_KT[MEMORD[m]], t_V[MEMORD[m]]]
            return ts

        def load_p3a_group(grp):
            q = "sync" if PRECAST else "pool"
            if grp == 0:
                wr = kv_toks([12, 13, 14, 15]) + [t_w3[0]]
                srcs = [(wo, wo_b if PRECAST else w_o, 8), (wa, wa_b if PRECAST else w_a, 4), (wb, wb_b if PRECAST else w_b, 4)]
                rd = [t_pc[0]]
            elif grp == 1:
                wr = kv_toks([8, 9, 10, 11]) + [t_w3[1]]
                srcs = [(wg, wg_b if PRECAST else w_in[:, 2048:4096], 8)]
                rd = [t_pc[1]]
            else:
                wr = kv_toks([4, 5, 6, 7]) + [t_w3[2], t_c3]
                srcs = [(wu, wu_b if PRECAST else w_in[:, 0:512], 8), (wp, wp_b if PRECAST else pool_w, 4)]
                rd = [t_pc[2]]
            for (dstv, srcap, nk) in srcs:
                for kc in range(nk):
                    P.emit(q, I("dma_start", out=dstv[:, kc, :], in_=srcap[kc * 128:(kc + 1) * 128, :]),
                           reads=rd, writes=wr, stream=f"lw{grp}")
            if grp == 2:
                P.emit("sync", I("dma_start", out=g1b3, in_=g1b), writes=wr, stream="lw2")
                P.emit("sync", I("dma_start", out=gbv, in_=gate_b), writes=wr, stream="lw2")
                P.emit("sync", I("dma_start", out=corrv, in_=corr), writes=wr, stream="lw2")

        tiles = []
        for s in (3, 2, 1, 0):
            units = slot_units(s)
            for h in range(4):
                lst = []
                for u, (kind, pos) in enumerate(units):
                    for t in range(4):
                        lst.append(dict(s=s, h=h, u=u, pos=pos, t=t, dg=(kind == "diag")))
                lst[0]["first"] = True
                lst[-1]["last"] = True
                tiles += lst
        NT = len(tiles)
        ptc = [0]
        sc = [0]

        def emit_qk(tl):
            s, h, u, pos, t, dg = tl["s"], tl["h"], tl["u"], tl["pos"], tl["t"], tl["dg"]
            q0 = s * 512
            c0 = 128 * t if dg else 0
            sp = sc[0] % 2
            sc[0] += 1
            for comp in range(2):
                pr = slice(64 * comp, 64 * comp + 64)
                bi = 2 * sp + comp
                P.emit("pe", I("matmul", bank(bi)[:, c0:512], lhsT=KTp[pos][pr, h, t * 128:(t + 1) * 128],
                               rhs=QT[pr, h, q0 + c0:q0 + 512], start=True, stop=True),
                       reads=[t_KT[pos], t_QT[s]], writes=[tb[bi]])
            pt = ptc[0] % NPT
            ptc[0] += 1
            spair = psum[:, 1024 * sp:1024 * (sp + 1)].rearrange("p (c q) -> p c q", c=2)
            ptv = PT[pt].rearrange("p (c q) -> p c q", c=2)
            if h == 0:
                for jj in range(2):
                    lo, hi = max(c0, 256 * jj), 256 * (jj + 1)
                    if lo >= hi:
                        continue
                    bc = bcol(s, u, t, 0, jj)
                    P.emit("act", I("activation", out=ptv[:, :, lo:hi], in_=spair[:, :, lo:hi],
                                    func=AF.Exp, scale=SCALE, bias=biasv[:, bc:bc + 1]),
                           reads=[tb[2 * sp], tb[2 * sp + 1], t_bias], writes=[t_PT[pt]])
            else:
                bc = bcol(s, u, t, h)
                P.emit("act", I("activation", out=ptv[:, :, c0:512], in_=spair[:, :, c0:512],
                                func=AF.Exp, scale=SCALE, bias=biasv[:, bc:bc + 1]),
                       reads=[tb[2 * sp], tb[2 * sp + 1], t_bias], writes=[t_PT[pt]])
            if dg:
                for comp in range(2):
                    cc = 512 * comp + c0
                    P.emit("pool", I("tensor_tensor", out=PT[pt][:, cc:cc + 128], in0=PT[pt][:, cc:cc + 128], in1=tri, op=ALU.mult),
                           reads=[t_PT[pt], t_const2], writes=[t_PT[pt]])
            tl["pt"] = pt
            tl["c0"] = c0

        def emit_pv(tl):
            pos, t, h = tl["pos"], tl["t"], tl["h"]
            pt, c0 = tl["pt"], tl["c0"]
            first = tl.get("first", False)
            last = tl.get("last", False)
            for comp in range(2):
                rhs = PT[pt][:, 512 * comp + c0:512 * comp + 512]
                P.emit("pe", I("matmul", bank(4 + comp)[:, c0:512], lhsT=Vp[pos][:, t, h * 128:(h + 1) * 128], rhs=rhs,
                               start=first, stop=last), reads=[t_V[pos], t_PT[pt]], writes=[tb[4 + comp]])
            for comp in range(2):
                rhs = PT[pt][:, 512 * comp + c0:512 * comp + 512]
                P.emit("pe", I("matmul", bank(6 + comp)[:, c0:512], lhsT=ones, rhs=rhs,
                               start=first, stop=last), reads=[t_const2, t_PT[pt]], writes=[tb[6 + comp]])

        def finalize_part1():
            P.emit("act", I("activation", out=fo1, in_=bank(4), func=AF.Copy), reads=[tb[4]], writes=[t_f[0]])
            P.emit("dve", I("tensor_copy", out=fz1, in_=bank(6)), reads=[tb[6]], writes=[t_f[2]])
            P.emit("act", I("activation", out=ft2, in_=bank(5), func=AF.Copy), reads=[tb[5]], writes=[t_f[1]])
            P.emit("dve", I("tensor_copy", out=fz2, in_=bank(7)), reads=[tb[7]], writes=[t_f[3]])
            P.emit("dve", I("reciprocal", out=fz1, in_=fz1), reads=[t_f[2]], writes=[t_f[2]])
            P.emit("dve", I("tensor_tensor", out=fo1, in0=fo1, in1=fz1, op=ALU.mult), reads=[t_f[2]], writes=[t_f[0]])
            P.emit("dve", I("reciprocal", out=fz2, in_=fz2), reads=[t_f[3]], writes=[t_f[3]])
            P.emit("dve", I("tensor_tensor", out=ft2, in0=ft2, in1=fz2, op=ALU.mult), reads=[t_f[3]], writes=[t_f[1]])
            P.emit("dve", I("scalar_tensor_tensor", out=fo, in0=ft2, scalar=small[:, 3:4], in1=fo1, op0=ALU.mult, op1=ALU.add),
                   reads=[t_f[0], t_f[1], t_small], writes=[t_f[4]])
            P.emit("pool", I("tensor_tensor", out=fsq, in0=fo, in1=fo, op=ALU.mult), reads=[t_f[4]], writes=[t_f[5]])

        def finalize_part2(s, h):
            sp = sc[0] % 2
            sc[0] += 1
            bi = 2 * sp
            P.emit("pe", I("matmul", bank(bi), lhsT=onesf, rhs=fsq, start=True, stop=True),
                   reads=[t_const2, t_f[5]], writes=[tb[bi]])
            P.emit("act", I("activation", out=frs, in_=bank(bi), func=AF.Ln, scale=1.0 / 128, bias=EPS),
                   reads=[tb[bi]], writes=[t_f[6]])
            P.emit("act", I("activation", out=frs, in_=frs, func=AF.Exp, scale=-0.5), reads=[t_f[6]], writes=[t_f[6]])
            P.emit("dve", I("scalar_tensor_tensor", out=onT[:, h, s * 512:(s + 1) * 512], in0=fo, scalar=small[:, 2:3],
                            in1=frs, op0=ALU.mult, op1=ALU.mult),
                   reads=[t_f[4], t_f[6], t_small], writes=[t_onT[s]])

        LAG = 2
        FDELAY = 10
        qp = 0
        sched = []
        for _ in range(LAG):
            if qp < NT:
                emit_qk(tiles[qp]); qp += 1
        for i in range(NT):
            if qp < NT:
                emit_qk(tiles[qp]); qp += 1
            tl = tiles[i]
            emit_pv(tl)
            if tl.get("last", False):
                finalize_part1()
                sched.append((i + FDELAY, tl["s"], tl["h"]))
                if tl["h"] == 3 and tl["s"] > 0:
                    load_p3a_group(3 - tl["s"])
            while sched and sched[0][0] <= i:
                _, s_, h_ = sched.pop(0)
                finalize_part2(s_, h_)
        while sched:
            _, s_, h_ = sched.pop(0)
            finalize_part2(s_, h_)

        def alias(tok, olds):
            for o in olds:
                for ins in ([o.w] if o.w is not None else []) + list(o.r.values()):
                    cur = tok.r.get(ins.stream)
                    if cur is None or cur.idx < ins.idx:
                        tok.r[ins.stream] = ins
            return tok

        P2_TMP = [t_bias, t_const2] + t_PT + t_f
        KV03 = kv_toks([0, 1, 2, 3])
        KV47 = kv_toks([4, 5, 6, 7])
        A3 = Alloc([(kvm(0), kvm(4)), (M47_FREE, kvm(8)), (KV_END, COMMON_END), (ONT_END, ARENA_BYTES)])
        o_x4 = [A3.get(4 * 1024 * 4) for _ in range(2)]
        o_hT3 = [A3.get(8 * 512 * 2) for _ in range(2)]
        o_merged = A3.get(8 * 512 * 2)
        o_pooled = A3.get(4 * 512 * 2)
        o_mixed = A3.get(4 * 512 * 2)
        o_hTh = [A3.get(8 * 128 * 2) for _ in range(2)]
        o_xh = A3.get(4096)
        o_hb3 = [A3.get(2048) for _ in range(2)]
        o_uT = A3.get(4 * 528 * 4)
        o_s2 = A3.get(4 * 528 * 4)
        o_s3 = A3.get(4 * 528 * 4)
        o_gt = [A3.get(2048) for _ in range(4)]
        o_tmpm = [A3.get(2048) for _ in range(2)]
        x4s = [view(o, [4, 1024], F32) for o in o_x4]
        hT3s = [view(o, [8, 512], BF16) for o in o_hT3]
        hThs = [view(o, [8, 128], BF16) for o in o_hTh]
        xh = view(o_xh, [1024], F32)
        hb3 = [view(o, [1024], BF16) for o in o_hb3]
        uT = view(o_uT, [4, 528], F32); s2 = view(o_s2, [4, 528], F32); s3 = view(o_s3, [4, 528], F32)
        pooled = view(o_pooled, [4, 512], BF16); mixed = view(o_mixed, [4, 512], BF16)
        gts = [view(o, [512], F32) for o in o_gt]
        tmpm = [view(o, [512], F32) for o in o_tmpm]
        merged = view(o_merged, [8, 512], BF16)
        t_x4 = [[alias(Tok("x4"), KV03) for _ in range(4)] for _ in range(2)]
        t_xh = alias(Tok("xh"), P2_TMP)
        t_hb3 = [alias(Tok("hb3"), P2_TMP) for _ in range(2)]
        t_hT3 = [alias(Tok("hT3"), KV47) for _ in range(2)]
        t_hTh = [alias(Tok("hTh"), KV47 + P2_TMP) for _ in range(2)]
        t_uT = alias(Tok("uT"), P2_TMP); t_s2 = alias(Tok("s2"), P2_TMP); t_s3 = alias(Tok("s3"), P2_TMP)
        t_pooled = alias(Tok("pooled"), t_QT); t_mixed = alias(Tok("mixed"), t_QT)
        t_gt = [alias(Tok("gt"), P2_TMP) for _ in range(4)]; t_tmpm = [alias(Tok("tmpm"), P2_TMP) for _ in range(2)]
        t_merged = alias(Tok("merged"), t_QT)
        t_x1d = [[Tok("x1d") for _ in range(4)] for _ in range(4)]
        if not PRECAST:
            pass
        hbc = [0]

        def p3a_norm_pieces(s):
            d = s % 2
            NA, NB_ = [], []
            st0 = {}

            def loads():
                for blk in range(4):
                    r0 = s * 512 + blk * 128
                    P.emit("sync", I("dma_start", out=x4s[d][:, blk, :], in_=xkv[r0:r0 + 128, :]),
                           writes=[t_x4[d][blk]], stream="ld_xa")

            def halo_a():
                P.emit("sync", I("dma_start", out=xh, in_=xhalo[s * 128:(s + 1) * 128, :]), writes=[t_xh], stream="ld_h")
                k = hbc[0] % 2
                hbc[0] += 1
                st0["k"] = k
                norm_A(xh, t_xh, hb3[k], t_hb3[k], g1b3, t_c3)

            def halo_b():
                k = st0["k"]
                norm_B(hb3[k], t_hb3[k], hThs[d][:, :, :], t_hTh[d], 0)
            NA.append(halo_a)
            NB_.append(halo_b)
            for blk in range(4):
                st = {}

                def pa(blk=blk, st=st):
                    k = hbc[0] % 2
                    hbc[0] += 1
                    st["k"] = k
                    norm_A(x4s[d][:, blk, :], t_x4[d][blk], hb3[k], t_hb3[k], g1b3, t_c3)

                def pb(blk=blk, st=st):
                    k = st["k"]
                    norm_B(hb3[k], t_hb3[k], hT3s[d][:, :, blk * 128:(blk + 1) * 128], t_hT3[d], blk + 1)
                NA.append(pa)
                NB_.append(pb)
            return NA, NB_, loads

        def p3a_comp_pieces(s):
            d = s % 2
            hT3 = hT3s[d]; hTh = hThs[d]; x4 = x4s[d]
            th3 = t_hT3[d]; thh = t_hTh[d]
            pcs = []

            def pool1():
                for g in range(4):
                    bi = nb()
                    for kc in range(8):
                        P.emit("pe", I("matmul", bank(bi)[:, 0:128], lhsT=wu[:, kc, g * 128:(g + 1) * 128],
                                       rhs=hTh[:, kc, :], start=(kc == 0), stop=(kc == 7)),
                               reads=[t_w3[2], thh], writes=[tb[bi]])
                    P.emit("act", I("activation", out=uT[:, g, 0:16], in_=bank(bi)[:, 112:128], func=AF.Copy),
                           reads=[tb[bi]], writes=[t_uT])
                    bi2 = nb()
                    for kc in range(8):
                        P.emit("pe", I("matmul", bank(bi2), lhsT=wu[:, kc, g * 128:(g + 1) * 128],
                                       rhs=hT3[:, kc, :], start=(kc == 0), stop=(kc == 7)),
                               reads=[t_w3[2], th3], writes=[tb[bi2]])
                    P.emit("dve", I("tensor_copy", out=uT[:, g, 16:528], in_=bank(bi2)), reads=[tb[bi2]], writes=[t_uT])
                for g in range(4):
                    cur, tcur = uT, t_uT
                    k = 1
                    bufs = [(s2, t_s2), (s3, t_s3)]
                    bi_ = 0
                    while k < POOL_W[g]:
                        dstb, tdst = bufs[bi_ % 2]
                        bi_ += 1
                        eng = "pool" if g % 2 == 0 else "dve"
                        P.emit(eng, I("tensor_tensor", out=dstb[:, g, k:528], in0=cur[:, g, k:528], in1=cur[:, g, 0:528 - k], op=ALU.add),
                               reads=[tcur], writes=[tdst])
                        cur, tcur = dstb, tdst
                        k *= 2
                    if s == 0:
                        P.emit("dve", I("tensor_tensor", out=cur[:, g, 16:32], in0=cur[:, g, 16:32], in1=corrv[:, g, :], op=ALU.mult),
                               reads=[tcur, t_c3], writes=[tcur])
                    P.emit("dve", I("scalar_tensor_tensor", out=pooled[:, g, :], in0=cur[:, g, 16:528], scalar=1.0 / POOL_W[g],
                                    in1=uT[:, g, 16:528], op0=ALU.mult, op1=ALU.subtract),
                           reads=[tcur, t_uT], writes=[t_pooled])

            def pool2():
                for g in range(4):
                    bi = nb()
                    P.emit("pe", I("matmul", bank(bi), lhsT=wp[:, g, :], rhs=pooled[:, g, :], start=True, stop=True),
                           reads=[t_w3[2], t_pooled], writes=[tb[bi]])
                    P.emit("act", I("activation", out=mixed[:, g, :], in_=bank(bi), func=AF.Copy, scale=small[:, 24 + g:25 + g]),
                           reads=[tb[bi], t_small], writes=[t_mixed])

            gate_banks = {}

            def gates_piece(fo_):
                def piece():
                    for br in range(2):
                        bg = nb()
                        c0 = br * 1024 + fo_ * 128
                        for kc in range(8):
                            P.emit("pe", I("matmul", bank(bg), lhsT=wg[:, kc, c0:c0 + 128], rhs=hT3[:, kc, :],
                                           start=(kc == 0), stop=(kc == 7)),
                                   reads=[t_w3[1], th3], writes=[tb[bg]])
                        gi = (fo_ % 2) * 2 + br
                        P.emit("act", I("activation", out=gts[gi], in_=bank(bg), func=AF.Sigmoid,
                                        bias=gbv[:, br * 8 + fo_:br * 8 + fo_ + 1]),
                               reads=[tb[bg], t_c3], writes=[t_gt[gi]])
                return piece

            def br_piece(fo_):
                def piece():
                    b_ya, b_yb = nb(), nb()
                    for g in range(4):
                        P.emit("pe", I("matmul", bank(b_ya), lhsT=wa[:, g, fo_ * 128:(fo_ + 1) * 128],
                                       rhs=mixed[:, g, :], start=(g == 0), stop=(g == 3)),
                               reads=[t_w3[0], t_mixed], writes=[tb[b_ya]])
                    for h in range(4):
                        P.emit("pe", I("matmul", bank(b_yb), lhsT=wb[:, h, fo_ * 128:(fo_ + 1) * 128],
                                       rhs=onT[:, h, s * 512:(s + 1) * 512], start=(h == 0), stop=(h == 3)),
                               reads=[t_w3[0], t_onT[s]], writes=[tb[b_yb]])
                    tm = fo_ % 2
                    g0i = (fo_ % 2) * 2
                    P.emit("dve", I("tensor_tensor", out=tmpm[tm], in0=bank(b_ya), in1=gts[g0i], op=ALU.mult),
                           reads=[tb[b_ya], t_gt[g0i]], writes=[t_tmpm[tm]])
                    P.emit("dve", I("tensor_tensor", out=gts[g0i + 1], in0=bank(b_yb), in1=gts[g0i + 1], op=ALU.mult),
                           reads=[tb[b_yb], t_gt[g0i + 1]], writes=[t_gt[g0i + 1]])
                    P.emit("pool", I("tensor_tensor", out=merged[:, fo_, :], in0=tmpm[tm], in1=gts[g0i + 1], op=ALU.add),
                           reads=[t_tmpm[tm], t_gt[g0i + 1]], writes=[t_merged])
                return piece

            pcs += [pool1, gates_piece(0), gates_piece(1), pool2]
            for fo_ in range(6):
                pcs += [br_piece(fo_), gates_piece(fo_ + 2)]
            pcs += [br_piece(6), br_piece(7)]

            def out_piece(blk):
                def piece():
                    for half in range(2):
                        bi = nb()
                        for kc in range(8):
                            P.emit("pe", I("matmul", bank(bi), lhsT=merged[:, kc, blk * 128:(blk + 1) * 128],
                                           rhs=wo[:, kc, half * 512:(half + 1) * 512], start=(kc == 0), stop=(kc == 7)),
                                   reads=[t_w3[0], t_merged], writes=[tb[bi]])
                        P.emit("dve", I("tensor_tensor", out=x4[:, blk, half * 512:(half + 1) * 512], in0=bank(bi),
                                        in1=x4[:, blk, half * 512:(half + 1) * 512], op=ALU.add),
                               reads=[tb[bi]], writes=[t_x4[d][blk]])
                    r0 = s * 512 + blk * 128
                    P.emit("sync", I("dma_start", out=x1s[r0:r0 + 128, :], in_=x4[:, blk, :]),
                           reads=[t_x4[d][blk]], writes=[t_x1d[s][blk]], stream="st_x1")
                return piece
            for blk in range(4):
                pcs.append(out_piece(blk))
            return pcs

        NA, NB_, lds = p3a_norm_pieces(3)
        lds()
        NA[0](); NA[1](); NB_[0](); NA[2](); NB_[1](); NA[3](); NB_[2](); NA[4](); NB_[3](); NB_[4]()
        for s in (3, 2, 1, 0):
            A = p3a_comp_pieces(s)
            if s > 0:
                NA, NB_, lds = p3a_norm_pieces(s - 1)
                lds()
                sched_AB(A, NA, NB_, [4, 6, 8, 10, 12], [8, 10, 12, 14, 16])
            else:
                for a in A:
                    a()

        P.barrier()
        A4 = Alloc([(CONST_END, ARENA_BYTES)])
        o_w1 = A4.get(8 * 4096 * 2)
        o_w2 = A4.get(32 * 1024 * 2)
        o_g2b = A4.get(4096)
        o_st = [A4.get(4096) for _ in range(2)]
        o_hb4 = [A4.get(2048) for _ in range(2)]
        o_h2T = [A4.get(8 * 512 * 2) for _ in range(2)]
        o_aT = A4.get(32 * 512 * 2)
        o_r = [A4.get(2048) for _ in range(2)]
        w1 = view(o_w1, [8, 4096], BF16); w2 = view(o_w2, [32, 1024], BF16)
        g2bv = view(o_g2b, [1024], F32)
        stg = [view(o, [1024], F32) for o in o_st]
        hb4 = [view(o, [1024], BF16) for o in o_hb4]
        h2Ts = [view(o, [8, 512], BF16) for o in o_h2T]
        aT = view(o_aT, [32, 512], BF16)
        rl = [view(o, [512], F32) for o in o_r]
        t_w1 = [Tok("w1") for _ in range(4)]; t_w2 = [Tok("w2") for _ in range(4)]; t_g2b = Tok("g2b")
        t_st = [Tok("st") for _ in range(2)]
        t_hb4 = [Tok("hb4") for _ in range(2)]
        t_h2T = [Tok("h2T") for _ in range(2)]; t_aT = [Tok("aT") for _ in range(32)]; t_rl = [Tok("rl") for _ in range(2)]
        P.emit("sync", I("dma_start", out=g2bv, in_=g2b), writes=[t_g2b], stream="ld_c")
        wq_ = "sync" if PRECAST else "pool"
        def load_ffn_weights(qs1, qs2):
            for q in qs1:
                if PRECAST:
                    for hk in range(2):
                        P.emit(wq_, I("dma_start", out=w1[:, hk * 4:hk * 4 + 4, q * 1024:(q + 1) * 1024],
                                      in_=w1_b[hk * 512:(hk + 1) * 512, q * 1024:(q + 1) * 1024].rearrange("(k p) c -> p k c", p=128)),
                               reads=[t_pc[3]], writes=[t_w1[q]], stream=f"lf{q}")
                else:
                    for kc in range(8):
                        P.emit(wq_, I("dma_start", out=w1[:, kc, q * 1024:(q + 1) * 1024],
                                      in_=w_1[kc * 128:(kc + 1) * 128, q * 1024:(q + 1) * 1024]),
                               writes=[t_w1[q]], stream=f"lf{q}")
            for q in qs2:
                if PRECAST:
                    for hk in range(2):
                        P.emit(wq_, I("dma_start", out=w2[:, q * 8 + hk * 4:q * 8 + hk * 4 + 4, :],
                                      in_=w2_b[q * 1024 + hk * 512:q * 1024 + (hk + 1) * 512, :].rearrange("(k p) c -> p k c", p=128)),
                               reads=[t_pc[3]], writes=[t_w2[q]], stream=f"lf{4 + q}")
                else:
                    for fc in range(q * 8, (q + 1) * 8):
                        P.emit(wq_, I("dma_start", out=w2[:, fc, :], in_=w_2[fc * 128:(fc + 1) * 128, :]),
                               writes=[t_w2[q]], stream=f"lf{4 + q}")

        stc = [0]
        last_out = [None]

        def p3b_norm_pieces(s):
            d = s % 2
            NA, NB_ = [], []
            for blk in range(4):
                st = {}

                def pa(blk=blk, st=st):
                    k = stc[0] % 2
                    stc[0] += 1
                    st["k"] = k
                    r0 = s * 512 + blk * 128
                    P.emit("sync", I("dma_start", out=stg[k], in_=x1s[r0:r0 + 128, :]),
                           reads=[t_x1d[s][blk]], writes=[t_st[k]], stream="ld_x1a" if k == 0 else "ld_x1b")
                    norm_A(stg[k], t_st[k], hb4[k], t_hb4[k], g2bv, t_g2b)

                def pb(blk=blk, st=st):
                    k = st["k"]
                    norm_B(hb4[k], t_hb4[k], h2Ts[d][:, :, blk * 128:(blk + 1) * 128], t_h2T[d], blk)
                NA.append(pa)
                NB_.append(pb)
            return NA, NB_

        def p3b_comp_pieces(s):
            d = s % 2
            h2T = h2Ts[d]
            pcs = []

            def ffn1(fg):
                def piece():
                    for fc in range(fg * 4, fg * 4 + 4):
                        bi = nb()
                        for kc in range(8):
                            P.emit("pe", I("matmul", bank(bi), lhsT=w1[:, kc, fc * 128:(fc + 1) * 128], rhs=h2T[:, kc, :],
                                           start=(kc == 0), stop=(kc == 7)),
                                   reads=[t_w1[fc // 8], t_h2T[d]], writes=[tb[bi]])
                        ri = fc % 2
                        P.emit("act", I("activation", out=rl[ri], in_=bank(bi), func=AF.Relu),
                               reads=[tb[bi]], writes=[t_rl[ri]])
                        P.emit("dve", I("scalar_tensor_tensor", out=aT[:, fc, :], in0=bank(bi), scalar=0.0, in1=rl[ri],
                                        op0=ALU.max, op1=ALU.mult),
                               reads=[tb[bi], t_rl[ri]], writes=[t_aT[fc]])
                return piece
            for fg in range(8):
                pcs.append(ffn1(fg))

            def ffn2(blk):
                def piece():
                    k = stc[0] % 2
                    stc[0] += 1
                    r0 = s * 512 + blk * 128
                    P.emit("sync", I("dma_start", out=stg[k], in_=x1s[r0:r0 + 128, :]),
                           reads=[t_x1d[s][blk]], writes=[t_st[k]], stream="ld_x1a" if k == 0 else "ld_x1b")
                    for half in range(2):
                        bi = nb()
                        for fc in range(32):
                            P.emit("pe", I("matmul", bank(bi), lhsT=aT[:, fc, blk * 128:(blk + 1) * 128],
                                           rhs=w2[:, fc, half * 512:(half + 1) * 512], start=(fc == 0), stop=(fc == 31)),
                                   reads=[t_w2[fc // 8], t_aT[fc]], writes=[tb[bi]])
                        P.emit("dve", I("tensor_tensor", out=stg[k][:, half * 512:(half + 1) * 512], in0=bank(bi),
                                        in1=stg[k][:, half * 512:(half + 1) * 512], op=ALU.add),
                               reads=[tb[bi]], writes=[t_st[k]])
                    last_out[0] = P.emit("sync", I("dma_start", out=out[r0:r0 + 128, :], in_=stg[k]),
                                         reads=[t_st[k]], stream="st_o")
                return piece
            for blk in range(4):
                pcs.append(ffn2(blk))
            return pcs

        NA, NB_ = p3b_norm_pieces(0)
        NA[0](); NA[1]()
        load_ffn_weights([0], [])
        NB_[0](); NA[2](); NB_[1](); NA[3]()
        load_ffn_weights([1, 2, 3], [0, 1, 2, 3])
        NB_[2](); NB_[3]()
        for s in range(4):
            A = p3b_comp_pieces(s)
            if s + 1 < 4:
                NA, NB_ = p3b_norm_pieces(s + 1)
                sched_AB(A, NA, NB_, [0, 1, 2, 3], [2, 3, 4, 5])
            else:
                for a in A:
                    a()

        run = P.finalize(nc, sems, {"sync": [last_out[0]]})

        @block.sync
        def _(e):
            run("sync")

        @block.scalar
        def _(e):
            run("act")

        @block.gpsimd
        def _(e):
            run("pool")

        @block.vector
        def _(e):
            run("dve")

        @block.tensor
        def _(e):
            run("pe")
    return nc


def core_layout(c):
    b, j = c // 4, c % 4
    own = [j, 7 - j, 8 + j, 15 - j]
    others = [x for x in range(16) if x not in own]
    return b, own, own + others


def make_bias(own, pi):
    tab = np.zeros((128, NBC), np.float32)
    p = np.arange(128, dtype=np.float64)
    for s in range(4):
        q0 = 512 * own[s]
        for u, (kind, pos) in enumerate(slot_units(s)):
            chunk = pi[pos]
            real = (chunk < own[s]) if kind != "diag" else True
            for t in range(4):
                kp = 512 * chunk + 128 * t + p
                for h in range(4):
                    if h == 0:
                        for jj in range(2):
                            v = SLOPES[0] * (kp - (q0 + 256 * jj + 128)) if real else NEG
                            tab[:, bcol(s, u, t, 0, jj)] = v if real else NEG
                    else:
                        v = SLOPES[h] * (kp - (q0 + 511)) if real else NEG
                        tab[:, bcol(s, u, t, h)] = np.minimum(v, 0.0) if real else NEG
    return tab


_NC_CACHE = {}


def kernel(x, norm1_g, w_in, gate_b, pool_w, pool_scale, q_norm_g, k_norm_g,
           lambda_q1, lambda_k1, lambda_q2, lambda_k2, subln_g,
           w_branch_a, w_branch_b, w_out, norm2_g, w_ff1, w_ff2):
    f = lambda a: np.ascontiguousarray(np.asarray(a, dtype=np.float32))
    x = f(x)
    if "nc" not in _NC_CACHE:
        _NC_CACHE["nc"] = build_nc()
    nc = _NC_CACHE["nc"]

    def cols(v, n):
        return np.ascontiguousarray(f(v).reshape(n, 128).T)

    ident = np.eye(128, dtype=np.float32)
    tri = (np.arange(128)[None, :] >= np.arange(128)[:, None]).astype(np.float32)
    blk = np.kron(np.eye(2, dtype=np.float32), np.ones((64, 64), np.float32))
    cst = np.ascontiguousarray(np.concatenate([ident, tri, blk], axis=1))
    lamrow = np.concatenate([f(lambda_q1)[0], f(lambda_k1)[0], f(lambda_q2)[0], f(lambda_k2)[0]])
    shared = {
        "w_in": f(w_in)[0], "gate_b": cols(f(gate_b)[0], 16), "pool_w": np.ascontiguousarray(f(pool_w)[0].reshape(512, 128)),
        "pool_scale": cols(f(pool_scale)[0], 4), "g1b": np.ascontiguousarray(np.tile(f(norm1_g)[0][None, :], (128, 1))),
        "g2b": np.ascontiguousarray(np.tile(f(norm2_g)[0][None, :], (128, 1))),
        "gq": np.ascontiguousarray(np.tile(f(q_norm_g)[0], 2)[:, None]),
        "gk": np.ascontiguousarray(np.tile(f(k_norm_g)[0], 2)[:, None]),
        "lam": np.ascontiguousarray(np.tile(lamrow[None, :], (128, 1))),
        "subg": np.ascontiguousarray(f(subln_g)[0][:, None]),
        "w_a": f(w_branch_a)[0], "w_b": f(w_branch_b)[0], "w_o": f(w_out)[0],
        "w_1": f(w_ff1)[0], "w_2": f(w_ff2)[0], "cst": cst,
    }
    in_maps = []
    layouts = []
    for c in range(NCORES):
        b, own, pi = core_layout(c)
        layouts.append((b, own))
        xb = x[b].reshape(16, 512, D)
        xkv = np.ascontiguousarray(xb[pi].reshape(SEQ, D))
        xhalo = np.zeros((4, 128, D), np.float32)
        for s in range(4):
            if own[s] > 0:
                xhalo[s] = x[b, own[s] * 512 - 128:own[s] * 512]
        corr = np.ones((4, 16), np.float32)
        if own[0] == 0:
            for g, w in enumerate(POOL_W):
                for t in range(16):
                    corr[g, t] = w / min(t + 1, w)
        m = dict(shared)
        m["xkv"] = xkv
        m["xhalo"] = np.ascontiguousarray(xhalo.reshape(512, D))
        m["biastab"] = make_bias(own, pi)
        m["corr"] = np.ascontiguousarray(np.tile(corr.reshape(1, 64), (128, 1)))
        in_maps.append(m)
    res = run_bass_kernel_spmd(nc, in_maps, core_ids=list(range(NCORES)))
    outp = np.zeros((2, SEQ, D), np.float32)
    for c in range(NCORES):
        b, own = layouts[c]
        o = np.asarray(res.results[c]["out"]).reshape(4, 512, D)
        for s in range(4):
            outp[b, own[s] * 512:(own[s] + 1) * 512] = o[s]
    return outp
```
